# Optimizing a Trainium2 kernel written in Bass

```python
import jax, jax.numpy as jnp
from jax import lax
import numpy as np

D_MODEL = 2048
BATCH = 4
SEQ = 4096
DEPTH = 1

CHUNK = 64
HEAD_DIM = 128
N_HEADS_SB = 8
N_HEADS_CA = 8
SB_WIDTH = N_HEADS_SB * HEAD_DIM
CA_WIDTH = N_HEADS_CA * HEAD_DIM
N_BRANCH = 2
Q_BLOCK = 128
N_PAST_CHUNKS = 8
BAND = (N_PAST_CHUNKS + 1) * CHUNK
REL_CLIP_PAST = 128
N_REL = REL_CLIP_PAST + CHUNK
D_FF = 5632
IN_COLS = 3 * SB_WIDTH + 3 * CA_WIDTH + N_BRANCH * D_MODEL
EPS = 1e-6

kernel_name = "hybrid_stickbreak_chunkrel_macaron_block"


def _rmsnorm(x, gain):
    xf = x.astype(jnp.float32)
    y = xf * lax.rsqrt(jnp.mean(xf * xf, axis=-1, keepdims=True) + EPS)
    return (y * gain.astype(jnp.float32)).astype(x.dtype)


def _swiglu(x, w_gate, w_up, w_down):
    return (jax.nn.silu(x @ w_gate) * (x @ w_up)) @ w_down


def _split_heads(t, n_heads):
    b, s, _ = t.shape
    return t.reshape(b, s, n_heads, HEAD_DIM).transpose(0, 2, 1, 3)


def _merge_heads(t):
    b, h, s, dh = t.shape
    return t.transpose(0, 2, 1, 3).reshape(b, s, h * dh)


def _stick_breaking_attention(q, k, v):
    seq = q.shape[2]
    scale = HEAD_DIM ** -0.5
    outs = []
    for blk in range(seq // Q_BLOCK):
        q0 = blk * Q_BLOCK
        kend = q0 + Q_BLOCK
        qb = q[:, :, q0:kend]
        kb = k[:, :, :kend]
        vb = v[:, :, :kend]
        z = jnp.einsum('bhqd,bhkd->bhqk', qb, kb).astype(jnp.float32) * scale
        t_pos = q0 + np.arange(Q_BLOCK)[:, None]
        s_pos = np.arange(kend)[None, :]
        causal = s_pos < t_pos
        log_keep = jnp.where(causal, jax.nn.log_sigmoid(-z), 0.0)
        later = lax.cumsum(log_keep, axis=3, reverse=True) - log_keep
        log_w = jax.nn.log_sigmoid(z) + later
        w = jnp.where(causal, jnp.exp(log_w), 0.0)
        outs.append(jnp.einsum('bhqk,bhkd->bhqd', w.astype(v.dtype), vb))
    return jnp.concatenate(outs, axis=2)


def _chunked_relpos_attention(q, k, v, rel_bias):
    b, h, seq, dh = q.shape
    nc = seq // CHUNK
    scale = HEAD_DIM ** -0.5
    qc = q.reshape(b, h, nc, CHUNK, dh)
    pad = ((0, 0), (0, 0), (N_PAST_CHUNKS * CHUNK, 0), (0, 0))
    kp = jnp.pad(k, pad).reshape(b, h, nc + N_PAST_CHUNKS, CHUNK, dh)
    vp = jnp.pad(v, pad).reshape(b, h, nc + N_PAST_CHUNKS, CHUNK, dh)
    band_idx = np.arange(nc)[:, None] + np.arange(N_PAST_CHUNKS + 1)[None, :]
    kband = kp[:, :, band_idx].reshape(b, h, nc, BAND, dh)
    vband = vp[:, :, band_idx].reshape(b, h, nc, BAND, dh)
    scores = jnp.einsum('bhcqd,bhckd->bhcqk', qc, kband).astype(jnp.float32) * scale
    dist = np.arange(CHUNK)[:, None] + N_PAST_CHUNKS * CHUNK - np.arange(BAND)[None, :]
    rel_idx = np.clip(dist, -(CHUNK - 1), REL_CLIP_PAST) + (CHUNK - 1)
    bias = rel_bias.astype(jnp.float32)[:, rel_idx]
    scores = scores + bias[None, :, None]
    src_chunk = np.arange(nc)[:, None] - N_PAST_CHUNKS + np.arange(BAND)[None, :] // CHUNK
    valid = src_chunk >= 0
    scores = jnp.where(valid[None, None, :, None, :], scores, -jnp.inf)
    p = jax.nn.softmax(scores, axis=-1)
    out = jnp.einsum('bhcqk,bhckd->bhcqd', p.astype(v.dtype), vband)
    return out.reshape(b, h, seq, dh)


def setup_inputs(seed: int = 0) -> dict:
    key = jax.random.key(seed)
    ks = jax.random.split(key, 20)
    nrm = jax.random.normal
    f32 = jnp.float32

    def gain(k, shape):
        return 1.0 + 0.02 * nrm(k, shape, f32)

    return {
        "x": nrm(ks[0], (BATCH, SEQ, D_MODEL), f32),
        "ffn1_norm": gain(ks[1], (DEPTH, D_MODEL)),
        "ffn1_w_gate": nrm(ks[2], (DEPTH, D_MODEL, D_FF), f32) * D_MODEL ** -0.5,
        "ffn1_w_up": nrm(ks[3], (DEPTH, D_MODEL, D_FF), f32) * D_MODEL ** -0.5,
        "ffn1_w_down": nrm(ks[4], (DEPTH, D_FF, D_MODEL), f32) * D_FF ** -0.5,
        "mix_norm": gain(ks[5], (DEPTH, D_MODEL)),
        "w_in": nrm(ks[6], (DEPTH, D_MODEL, IN_COLS), f32) * D_MODEL ** -0.5,
        "b_gate": 0.05 * nrm(ks[7], (DEPTH, N_BRANCH * D_MODEL), f32),
        "q_norm_ca": gain(ks[8], (DEPTH, HEAD_DIM)),
        "k_norm_ca": gain(ks[9], (DEPTH, HEAD_DIM)),
        "rel_bias": 0.2 * nrm(ks[10], (DEPTH, N_HEADS_CA, N_REL), f32),
        "w_o_sb": nrm(ks[11], (DEPTH, SB_WIDTH, D_MODEL), f32) * SB_WIDTH ** -0.5,
        "w_o_ca": nrm(ks[12], (DEPTH, CA_WIDTH, D_MODEL), f32) * CA_WIDTH ** -0.5,
        "w_out": nrm(ks[13], (DEPTH, D_MODEL, D_MODEL), f32) * D_MODEL ** -0.5,
        "ffn2_norm": gain(ks[14], (DEPTH, D_MODEL)),
        "ffn2_w_gate": nrm(ks[15], (DEPTH, D_MODEL, D_FF), f32) * D_MODEL ** -0.5,
        "ffn2_w_up": nrm(ks[16], (DEPTH, D_MODEL, D_FF), f32) * D_MODEL ** -0.5,
        "ffn2_w_down": nrm(ks[17], (DEPTH, D_FF, D_MODEL), f32) * D_FF ** -0.5,
        "final_norm": gain(ks[18], (DEPTH, D_MODEL)),
    }


def reference(x, ffn1_norm, ffn1_w_gate, ffn1_w_up, ffn1_w_down, mix_norm, w_in, b_gate,
              q_norm_ca, k_norm_ca, rel_bias, w_o_sb, w_o_ca, w_out, ffn2_norm,
              ffn2_w_gate, ffn2_w_up, ffn2_w_down, final_norm):
    b, s, _ = x.shape
    split_at = [SB_WIDTH, 2 * SB_WIDTH, 3 * SB_WIDTH,
                3 * SB_WIDTH + CA_WIDTH, 3 * SB_WIDTH + 2 * CA_WIDTH, 3 * SB_WIDTH + 3 * CA_WIDTH]
    for l in range(DEPTH):
        x = x + 0.5 * _swiglu(_rmsnorm(x, ffn1_norm[l]), ffn1_w_gate[l], ffn1_w_up[l], ffn1_w_down[l])

        hn = _rmsnorm(x, mix_norm[l])
        proj = hn @ w_in[l]
        q_sb, k_sb, v_sb, q_ca, k_ca, v_ca, gate_pre = jnp.split(proj, split_at, axis=-1)

        y_sb = _stick_breaking_attention(_split_heads(q_sb, N_HEADS_SB),
                                         _split_heads(k_sb, N_HEADS_SB),
                                         _split_heads(v_sb, N_HEADS_SB))
        y_sb = _merge_heads(y_sb) @ w_o_sb[l]

        qh = _rmsnorm(_split_heads(q_ca, N_HEADS_CA), q_norm_ca[l])
        kh = _rmsnorm(_split_heads(k_ca, N_HEADS_CA), k_norm_ca[l])
        y_ca = _chunked_relpos_attention(qh, kh, _split_heads(v_ca, N_HEADS_CA), rel_bias[l])
        y_ca = _merge_heads(y_ca) @ w_o_ca[l]

        gates = jax.nn.sigmoid((gate_pre + b_gate[l]).astype(jnp.float32))
        gates = gates.reshape(b, s, N_BRANCH, D_MODEL).astype(x.dtype)
        merged = gates[:, :, 0] * y_sb + gates[:, :, 1] * y_ca
        x = x + merged @ w_out[l]

        x = x + 0.5 * _swiglu(_rmsnorm(x, ffn2_norm[l]), ffn2_w_gate[l], ffn2_w_up[l], ffn2_w_down[l])

        x = _rmsnorm(x, final_norm[l])
    return x
```

```python
import numpy as np
import ml_dtypes
import concourse.bass as bass
import concourse.mybir as mybir
from concourse.bass_utils import run_bass_kernel_spmd

F32 = mybir.dt.float32
BF16 = mybir.dt.bfloat16
AF = mybir.ActivationFunctionType
ALU = mybir.AluOpType
NEG = -30000.0


class Cfg:
    def __init__(self, D=2048, DFF=5632, S=4096, NH=8, TT=1024, EPS=1e-6):
        self.D, self.DFF, self.S, self.NH, self.TT, self.EPS = D, DFF, S, NH, TT, EPS
        self.KC = D // 128
        self.NB = S // 128
        self.NOWN = self.NB // 2
        self.SOWN = S // 2
        self.NT = S // TT
        self.NT_OWN = self.SOWN // TT
        self.HW = NH * 128
        self.INC = 6 * self.HW + 2 * D
        self.GF = DFF // 256
        self.QB = min(4, self.NOWN)
        self.NQG = self.NOWN // self.QB
        self.NTG = TT // 512


class Sched:
    ENGS = ["pe", "act", "dve", "pool", "sp"]

    def __init__(self, nc, stack, n_dma_sems=8):
        self.nc = nc
        self.ops = []
        self.emitted = 0
        self.lastw = {}
        self.readers = {}
        self.n_dma_sems = n_dma_sems
        self.dma_rr = {e: 0 for e in self.ENGS}
        self.dma_cnt = {}
        self.dma_last = {}
        self.last_on_eng = {}
        self.cnt = {e: 0 for e in self.ENGS}
        self.seen = {e: {} for e in self.ENGS}
        self.csem = {e: stack.enter_context(nc.semaphore("c_" + e)) for e in self.ENGS}
        self.dsem = {}
        for e in ("sp", "pool"):
            for i in range(n_dma_sems):
                self.dsem[(e, i)] = stack.enter_context(nc.semaphore("d_%s%d" % (e, i)))

    def op(self, eng, meth, kw, reads=(), writes=(), dma=False):
        oid = len(self.ops)
        deps = set()
        for k in reads:
            w = self.lastw.get(k)
            if w is not None:
                deps.add(w)
        for k in writes:
            w = self.lastw.get(k)
            if w is not None:
                deps.add(w)
            for r in self.readers.get(k, ()):
                deps.add(r)
        for k in writes:
            self.lastw[k] = oid
            self.readers[k] = []
        for k in reads:
            self.readers.setdefault(k, []).append(oid)
        deps.discard(oid)
        o = dict(id=oid, eng=eng, meth=meth, kw=kw, deps=deps, dma=dma, sig=None, needed=False)
        if dma:
            i = self.dma_rr[eng]
            self.dma_rr[eng] = (i + 1) % self.n_dma_sems
            key = (eng, i)
            prev = self.dma_last.get(key)
            if prev is not None:
                deps.add(prev)
            self.dma_cnt[key] = self.dma_cnt.get(key, 0) + 1
            o["dsem"] = key
            o["dval"] = 16 * self.dma_cnt[key]
            self.dma_last[key] = oid
        self.ops.append(o)
        self.last_on_eng[eng] = oid
        return o

    def barrier(self):
        deps = set(v for v in self.last_on_eng.values() if v >= self.emitted) | set(self.dma_last.values())
        self.lastw = {}
        self.readers = {}
        b0 = self.op("sp", "nop", {}, writes=["__bar__"])
        b0["deps"] |= deps
        b0["deps"].discard(b0["id"])
        for e in self.ENGS:
            if e != "sp":
                self.op(e, "nop", {}, reads=["__bar__"])
        self.lastw = {}
        self.readers = {}

    def emit(self):
        nc = self.nc
        ops = self.ops
        new = ops[self.emitted:]
        self.emitted = len(ops)

        def pe_pe(p, o):
            return p["eng"] == "pe" and o["eng"] == "pe" and not o["dma"] and not p["dma"]

        for o in new:
            for d in o["deps"]:
                p = ops[d]
                if p["dma"] or pe_pe(p, o):
                    continue
                assert p["sig"] is not None or d >= len(ops) - len(new), "dep on already-emitted unsignalled op"
                p["needed"] = True
        for o in new:
            if (not o["dma"]) and o["needed"]:
                self.cnt[o["eng"]] += 1
                o["sig"] = self.cnt[o["eng"]]
        by_eng = {e: [o for o in new if o["eng"] == e] for e in self.ENGS}
        csem, dsem = self.csem, self.dsem

        def run(ename, eng):
            seen = self.seen[ename]
            for o in by_eng[ename]:
                need = {}
                for d in o["deps"]:
                    p = ops[d]
                    if p["dma"]:
                        s, v = dsem[p["dsem"]], p["dval"]
                    else:
                        if pe_pe(p, o):
                            continue
                        s, v = csem[p["eng"]], p["sig"]
                    if v > need.get(s, 0):
                        need[s] = v
                for s, v in need.items():
                    if v > seen.get(s, 0):
                        eng.wait_ge(s, v)
                        seen[s] = v
                ins = getattr(eng, o["meth"])(**o["kw"])
                if o["dma"]:
                    ins.then_inc(dsem[o["dsem"]], 16)
                elif o["sig"] is not None:
                    ins.then_inc(csem[ename], 1)

        with nc.Block() as block:
            block.tensor(lambda e: run("pe", e))
            block.scalar(lambda e: run("act", e))
            block.vector(lambda e: run("dve", e))
            block.gpsimd(lambda e: run("pool", e))
            block.sync(lambda e: run("sp", e))


def build(cfg, debug=False):
    import contextlib
    c = cfg
    D, DFF, S, NH, TT, KC = c.D, c.DFF, c.S, c.NH, c.TT, c.KC
    NOWN, SOWN, HW, NTG = c.NOWN, c.SOWN, c.HW, c.NTG
    nc = bass.Bass("TRN2", target_bir_lowering=False)

    def din(name, shape, dt=F32):
        return nc.dram_tensor(name, list(shape), dt, kind="ExternalInput").ap()

    scratch_kind = "ExternalOutput" if debug else "Internal"

    def dsc(name, shape, dt=BF16):
        return nc.dram_tensor(name, list(shape), dt, kind=scratch_kind).ap()

    xall = din("xall", [S, D])
    w = {}
    for pre in ("ffn1", "ffn2"):
        w[pre + "_g"] = din(pre + "_w_gate", [D, DFF])
        w[pre + "_u"] = din(pre + "_w_up", [D, DFF])
        w[pre + "_d"] = din(pre + "_w_down", [DFF, D])
    w_in = din("w_in", [D, c.INC])
    w_o_sb = din("w_o_sb", [HW, D])
    w_o_ca = din("w_o_ca", [HW, D])
    w_out = din("w_out", [D, D])
    gains_d = din("gains", [128, 4 * KC])
    bgate_d = din("bgate", [128, 2 * KC])
    qkg_d = din("qkg", [128, 2])
    biasT_d = din("biasT", [128, NH * 5 * 128])
    maskT_d = din("maskT", [128, 5 * 128])
    pad_d = din("padmask", [128, 128])
    cst_d = din("consts", [128, 5 * 128])
    out_d = nc.dram_tensor("out", [SOWN, D], F32, kind="ExternalOutput").ap()

    kTsb_d = dsc("kTsb", [NH, 128, S])
    kTca_d = dsc("kTca", [NH, 128, S])
    vsb_d = dsc("vsb", [S, HW])
    vca_d = dsc("vca", [S, HW])
    qTsb_d = dsc("qTsb", [NH, 128, SOWN])
    qTca_d = dsc("qTca", [NH, 128, SOWN])
    gates_d = dsc("gates", [2 * KC, 128, SOWN])
    x1T_d = dsc("x1T", [KC, 128, SOWN], F32)
    mrg_d = dsc("mrg", [KC, 128, SOWN])
    ysb_d = dsc("ysbd", [NH, 128, SOWN]) if debug else None
    yca_d = dsc("ycad", [NH, 128, SOWN]) if debug else None

    with contextlib.ExitStack() as top:
        sc = Sched(nc, top)

        def sb(name, shape, dt):
            return top.enter_context(nc.sbuf_tensor(name, list(shape), dt))

        pp = [top.enter_context(nc.psum_tensor("pp%d" % i, [128, 1024], F32)) for i in range(4)]
        ps = [pp[i // 2][:, (i % 2) * 512:(i % 2 + 1) * 512] for i in range(8)]
        PS = lambda i: ("ps", i)

        rot = {}

        def nxt(name, n):
            v = rot.get(name, 0)
            rot[name] = (v + 1) % n
            return v

        def mm(out, lhsT, rhs, start, stop, reads, writes):
            sc.op("pe", "matmul", dict(out=out, lhsT=lhsT, rhs=rhs, start=start, stop=stop), reads, writes)

        def act(out, in_, func, reads, writes, **kw):
            sc.op("act", "activation", dict(out=out, in_=in_, func=func, **kw), reads, writes)

        def tt(eng, out, in0, in1, op, reads, writes):
            sc.op(eng, "tensor_tensor", dict(out=out, in0=in0, in1=in1, op=op), reads, writes)

        def stt(out, in0, scalar, in1, op0, op1, reads, writes):
            sc.op("dve", "scalar_tensor_tensor", dict(out=out, in0=in0, scalar=scalar, in1=in1, op0=op0, op1=op1), reads, writes)

        def dma(q, out, in_, reads, writes):
            sc.op(q, "dma_start", dict(out=out, in_=in_), reads, writes, dma=True)

        cst_f = sb("cst_f", [128, 5 * 128], F32)
        ident_f = cst_f[:, 0:128]
        cst_b = sb("cst_b", [128, 5 * 128], BF16)
        ident_b, ones_b, negU_b, tri_b, nones_b = (cst_b[:, i * 128:(i + 1) * 128] for i in range(5))
        gains = sb("gains_sb", [128, 4 * KC], F32)
        bgate = sb("bgate_sb", [128, 2 * KC], F32)
        qkg = sb("qkg_sb", [128, 2], F32)
        qkg_s = sb("qkg_s", [128, 2], F32)
        epsD = sb("epsD", [128, 1], F32)
        one_t = sb("one_t", [128, 1], F32)
        pad_b = sb("pad_b", [128, 128], BF16)
        bm_b = sb("bm_b", [128, NH * 5 * 128], BF16)

        dma("sp", cst_f[:], cst_d, [], ["cst_f"])
        dma("sp", gains[:], gains_d, [], ["gains"])
        dma("sp", bgate[:], bgate_d, [], ["bgate"])
        dma("sp", qkg[:], qkg_d, [], ["qkg"])
        dma("pool", pad_b[:], pad_d, [], ["pad_b"])
        sc.op("dve", "tensor_copy", dict(out=cst_b[:], in_=cst_f[:]), ["cst_f"], ["cst_b"])
        sc.op("dve", "memset", dict(ap=epsD[:], constant=c.EPS), [], ["epsD"])
        sc.op("dve", "memset", dict(ap=one_t[:], constant=1.0), [], ["one_t"])
        sc.op("dve", "tensor_scalar", dict(out=qkg_s[:, 0:1], in0=qkg[:, 0:1], scalar1=128.0 ** -0.5, scalar2=None, op0=ALU.mult),
              ["qkg"], ["qkg_s0"])
        sc.op("dve", "tensor_copy", dict(out=qkg_s[:, 1:2], in_=qkg[:, 1:2]), ["qkg"], ["qkg_s1"])

        def tile_bufs(stack, sfx):
            def sa(name, shape, dt):
                return stack.enter_context(nc.sbuf_tensor(name + sfx, list(shape), dt))
            B = {}
            B["xT"] = sa("xT", [128, KC, TT], F32)
            B["xn"] = sa("xn", [128, KC, TT], BF16)
            B["wg"] = [sa("wg%d" % i, [128, KC, 256], BF16) for i in range(2)]
            B["wu"] = [sa("wu%d" % i, [128, KC, 256], BF16) for i in range(2)]
            B["wd"] = [sa("wd%d" % i, [128, 2, D], BF16) for i in range(2)]
            B["hb"] = [sa("hb%d" % i, [128, 2, TT], BF16) for i in range(2)]
            B["xio"] = [sa("xio%d" % i, [128, D], F32) for i in range(2)]
            B["sq"] = [sa("sq%d" % i, [128, 512], BF16) for i in range(2)]
            B["rs"] = sa("rs", [128, 512], F32)
            B["rsn"] = [sa("rsn%d" % i, [128, 512], F32) for i in range(2)]
            B["sil"] = [sa("sil%d" % i, [128, 512], F32) for i in range(2)]
            B["ost"] = [sa("ost%d" % i, [128, 512], BF16) for i in range(2)]
            B["vst"] = sa("vst", [128, TT // 128, 256], BF16)
            return B

        XT_ALL = [("xT", m, tg) for m in range(KC) for tg in range(NTG)]
        XN_ALL = [("xn", k, tg) for k in range(KC) for tg in range(NTG)]

        xbuf = {}

        def issue_x(B, t, blk):
            b = nxt("xio", 2)
            xbuf[(t, blk)] = b
            r0 = t * TT + blk * 128
            dma("sp", B["xio"][b][:], xall[r0:r0 + 128, :], [], [("xio", b)])

        def transpose_blk(B, t, blk):
            xT, xio = B["xT"], B["xio"]
            b = xbuf[(t, blk)]
            tgk = (blk * 128) // 512
            for q0 in range(0, KC, 4):
                n = min(4, KC - q0)
                pb = 6 + nxt("trps", 2)
                for i in range(n):
                    sc.op("pe", "transpose", dict(out=ps[pb][:, i * 128:(i + 1) * 128],
                                                  in_=xio[b][:, (q0 + i) * 128:(q0 + i + 1) * 128], identity=ident_f),
                          [("xio", b), "cst_f"], [PS(pb)])
                src = ps[pb][:, 0:n * 128].rearrange("p (a b) -> p a b", a=n)
                dst = xT[:, q0:q0 + n, blk * 128:(blk + 1) * 128]
                wr = [("xT", m, tgk) for m in range(q0, q0 + n)]
                if nxt("trev", 2) == 0:
                    act(dst, src, AF.Copy, [PS(pb)], wr)
                else:
                    sc.op("dve", "tensor_copy", dict(out=dst, in_=src), [PS(pb)], wr)

        def load_steps(B, t):
            nb_ = TT // 128

            def mk(k):
                def f():
                    transpose_blk(B, t, k)
                    if k + 2 < nb_:
                        issue_x(B, t, k + 2)
                return f
            return [mk(k) for k in range(nb_)]

        def load_prologue(B, t):
            issue_x(B, t, 0)
            if TT // 128 > 1:
                issue_x(B, t, 1)

        def load_transpose(B, t):
            load_prologue(B, t)
            for f in load_steps(B, t):
                f()

        def rmsnorm(B, which, to_xn):
            xT, xn, sq, rs = B["xT"], B["xn"], B["sq"], B["rs"]
            for tg in range(NTG):
                sl = slice(tg * 512, (tg + 1) * 512)
                for k in range(KC):
                    s_ = nxt("sq", 2)
                    act(sq[s_][:], xT[:, k, sl], AF.Square, [("xT", k, tg)], [("sq", s_)])
                    mm(ps[1][:], ones_b, sq[s_][:], k == 0, k == KC - 1, [("sq", s_), "cst_b"], [PS(1)])
                act(rs[:], ps[1][:], AF.Sqrt, [PS(1), "epsD"], ["rs"], scale=1.0 / D, bias=epsD[:, 0:1])
                sc.op("dve", "reciprocal", dict(out=rs[:], in_=rs[:]), ["rs"], ["rs"])
                for k in range(KC):
                    gcol = gains[:, which * KC + k: which * KC + k + 1]
                    if to_xn:
                        stt(xn[:, k, sl], xT[:, k, sl], gcol, rs[:], ALU.mult, ALU.mult, [("xT", k, tg), "rs", "gains"], [("xn", k, tg)])
                    else:
                        stt(xT[:, k, sl], xT[:, k, sl], gcol, rs[:], ALU.mult, ALU.mult, [("xT", k, tg), "rs", "gains"], [("xT", k, tg)])

        wcnt = {"g": 0}

        def ffn(B, Wg, Wu, Wd):
            xT, xn, wg, wu, wd, hb, sil = B["xT"], B["xn"], B["wg"], B["wu"], B["wd"], B["hb"], B["sil"]

            def down(g, slot, pairs):
                hp = g % 2
                for (m, tg) in pairs:
                    sl = slice(tg * 512, (tg + 1) * 512)
                    pb = 6 + nxt("dps", 2)
                    for cc in range(2):
                        mm(ps[pb][:], wd[slot][:, cc, m * 128:(m + 1) * 128], hb[hp][:, cc, sl], cc == 0, cc == 1,
                           [("wd", slot), ("hb", hp, cc, tg)], [PS(pb)])
                    stt(xT[:, m, sl], ps[pb][:], 0.5, xT[:, m, sl], ALU.mult, ALU.add, [PS(pb), ("xT", m, tg)], [("xT", m, tg)])

            all_pairs = [(m, tg) for m in range(KC) for tg in range(NTG)]
            combos = [(cc, tg) for cc in range(2) for tg in range(NTG)]
            npart = len(combos)
            parts = [all_pairs[i * len(all_pairs) // npart:(i + 1) * len(all_pairs) // npart] for i in range(npart)]
            prev = None
            for g in range(c.GF):
                slot = wcnt["g"] % 2
                wcnt["g"] += 1
                c0 = g * 256
                dma("pool", wg[slot][:], Wg[:, c0:c0 + 256].rearrange("(k p) n -> p k n", p=128), [], [("wg", slot)])
                dma("pool", wu[slot][:], Wu[:, c0:c0 + 256].rearrange("(k p) n -> p k n", p=128), [], [("wu", slot)])
                dma("pool", wd[slot][:], Wd[c0:c0 + 256, :].rearrange("(k p) n -> p k n", p=128), [], [("wd", slot)])
                for ci, (cc, tg) in enumerate(combos):
                    sl = slice(tg * 512, (tg + 1) * 512)
                    r = nxt("gups", 2)
                    pg, pu = 2 + r, 4 + r
                    dl = list(parts[ci]) if prev is not None else []
                    nmm = 2 * KC
                    every = max(1, nmm // max(1, len(dl))) if dl else 0
                    cnt = 0
                    for (pbk, wsrc, wkey) in ((pg, wg, "wg"), (pu, wu, "wu")):
                        for k in range(KC):
                            mm(ps[pbk][:], wsrc[slot][:, k, cc * 128:(cc + 1) * 128], xn[:, k, sl], k == 0, k == KC - 1,
                               [(wkey, slot), ("xn", k, tg)], [PS(pbk)])
                            cnt += 1
                            if dl and cnt % every == 0:
                                down(prev[0], prev[1], [dl.pop(0)])
                    act(sil[r][:], ps[pg][:], AF.Silu, [PS(pg)], [("sil", r)])
                    tt("dve", hb[g % 2][:, cc, sl], sil[r][:], ps[pu][:], ALU.mult, [("sil", r), PS(pu)], [("hb", g % 2, cc, tg)])
                    if dl:
                        down(prev[0], prev[1], dl)
                prev = (g, slot)
            down(prev[0], prev[1], all_pairs)

        def wload(B, Wap, col0):
            r = wcnt["r"] = (wcnt.get("r", -1) + 1) % 4
            name, slot = ("wg", r) if r < 2 else ("wu", r - 2)
            tile_ = B[name][slot]
            dma("pool", tile_[:], Wap[:, col0:col0 + 256].rearrange("(k p) n -> p k n", p=128), [], [(name, slot)])
            return tile_, (name, slot)

        pend = []

        def flush_pend():
            while pend:
                f, a = pend.pop(0)
                f(*a)

        def fm_group(B, Wap, col0, evac):
            xn = B["xn"]
            wt_, wkey = wload(B, Wap, col0)
            for cc in range(2):
                for tg in range(NTG):
                    sl = slice(tg * 512, (tg + 1) * 512)
                    pb = 2 + nxt("fmps", 4)
                    for k in range(KC):
                        mm(ps[pb][:], wt_[:, k, cc * 128:(cc + 1) * 128], xn[:, k, sl], k == 0, k == KC - 1,
                           [wkey, ("xn", k, tg)], [PS(pb)])
                    flush_pend()
                    pend.append((evac, (cc, tg, pb)))

        def win_phase(B, t, filler=None):
            xn, wg, sq, rs, ost, vst = B["xn"], B["wg"], B["sq"], B["rs"], B["ost"], B["vst"]
            own = t < c.NT_OWN
            s0 = t * TT

            def plain(dst, scale):
                def ev(gidx):
                    def f(cc, tg, pb):
                        head = gidx * 2 + cc
                        ob = nxt("ost", 2)
                        act(ost[ob][:], ps[pb][:], AF.Copy, [PS(pb)], [("ost", ob)], scale=scale)
                        dma("sp", dst[head, :, s0 + tg * 512: s0 + (tg + 1) * 512], ost[ob][:], [("ost", ob)], ["dram_qk"])
                    return f
                return ev

            def normed(dst, gi_):
                def ev(gidx):
                    def f(cc, tg, pb):
                        head = gidx * 2 + cc
                        s_ = nxt("sq", 2)
                        rb = nxt("rsn", 2)
                        rsb = B["rsn"][rb]
                        act(sq[s_][:], ps[pb][:], AF.Square, [PS(pb)], [("sq", s_)])
                        mm(ps[rb][:], ones_b, sq[s_][:], True, True, [("sq", s_), "cst_b"], [PS(rb)])
                        act(rsb[:], ps[rb][:], AF.Sqrt, [PS(rb), "epsD"], [("rsn", rb)], scale=1.0 / 128, bias=epsD[:, 0:1])
                        sc.op("dve", "reciprocal", dict(out=rsb[:], in_=rsb[:]), [("rsn", rb)], [("rsn", rb)])
                        ob = nxt("ost", 2)
                        stt(ost[ob][:], ps[pb][:], qkg_s[:, gi_:gi_ + 1], rsb[:], ALU.mult, ALU.mult,
                            [PS(pb), ("rsn", rb), "qkg_s0", "qkg_s1"], [("ost", ob)])
                        dma("sp", dst[head, :, s0 + tg * 512: s0 + (tg + 1) * 512], ost[ob][:], [("ost", ob)], ["dram_qk"])
                    return f
                return ev

            def gate_ev(gidx):
                def f(cc, tg, pb):
                    ch = gidx * 2 + cc
                    ob = nxt("ost", 2)
                    act(ost[ob][:], ps[pb][:], AF.Sigmoid, [PS(pb), "bgate"], [("ost", ob)], bias=bgate[:, ch:ch + 1])
                    dma("sp", gates_d[ch, :, s0 + tg * 512: s0 + (tg + 1) * 512], ost[ob][:], [("ost", ob)], ["dram_gates"])
                return f

            sections = []
            if own:
                sections.append((0, HW // 256, plain(qTsb_d, 128.0 ** -0.5)))
            sections.append((HW, HW // 256, plain(kTsb_d, 1.0)))
            if own:
                sections.append((3 * HW, HW // 256, normed(qTca_d, 0)))
            sections.append((4 * HW, HW // 256, normed(kTca_d, 1)))
            if own:
                sections.append((6 * HW, 2 * D // 256, gate_ev))
            ngroups = sum(x[1] for x in sections) + 2 * (HW // 256)
            fill_every = max(1, ngroups // (TT // 128 + 1))
            gcount = [0]

            def fill():
                gcount[0] += 1
                if filler and gcount[0] % fill_every == 0:
                    filler.pop(0)()

            for (cbase, ng, evf) in sections:
                for gi in range(ng):
                    fm_group(B, w_in, cbase + gi * 256, evf(gi))
                    fill()
            flush_pend()
            for (cbase, vd) in ((2 * HW, vsb_d), (5 * HW, vca_d)):
                for gi in range(HW // 256):
                    wt_, wkey = wload(B, w_in, cbase + gi * 256)
                    for tb in range(TT // 128):
                        pb = 2 + nxt("fmps", 4)
                        tgk = (tb * 128) // 512
                        for k in range(KC):
                            mm(ps[pb][:, 0:256], xn[:, k, tb * 128:(tb + 1) * 128], wt_[:, k, :], k == 0, k == KC - 1,
                               [wkey, ("xn", k, tgk)], [PS(pb)])
                        if nxt("vev", 2) == 0:
                            act(vst[:, tb, :], ps[pb][:, 0:256], AF.Copy, [PS(pb)], [("vst", tb)])
                        else:
                            sc.op("dve", "tensor_copy", dict(out=vst[:, tb, :], in_=ps[pb][:, 0:256]), [PS(pb)], [("vst", tb)])
                    dst = vd[s0:s0 + TT, gi * 256:(gi + 1) * 256].rearrange("(b p) n -> p b n", p=128)
                    dma("sp", dst, vst[:], [("vst", tb) for tb in range(TT // 128)], ["dram_v"])
                    fill()
            while filler:
                filler.pop(0)()

        with contextlib.ExitStack() as phA:
            B = tile_bufs(phA, "_a")
            load_transpose(B, 0)
            for t in range(c.NT):
                rmsnorm(B, 0, True)
                ffn(B, w["ffn1_g"], w["ffn1_u"], w["ffn1_d"])
                if t < c.NT_OWN:
                    dma("sp", x1T_d[:, :, t * TT:(t + 1) * TT].rearrange("k p s -> p k s"), B["xT"][:], XT_ALL, ["dram_x1"])
                rmsnorm(B, 1, True)
                filler = []
                if t + 1 < c.NT:
                    load_prologue(B, t + 1)
                    filler = load_steps(B, t + 1)
                win_phase(B, t, filler)
            sc.barrier()
            sc.emit()

        with contextlib.ExitStack() as phB:
            def sbb(name, shape, dt):
                return phB.enter_context(nc.sbuf_tensor(name, list(shape), dt))

            ysb = sbb("ysb", [128, NH, SOWN], BF16)
            yca = sbb("yca", [128, NH, SOWN], BF16)
            with contextlib.ExitStack() as phB1:
                def sb1(name, shape, dt):
                    return phB1.enter_context(nc.sbuf_tensor(name, list(shape), dt))

                kT = [sb1("kT%d" % i, [128, S], BF16) for i in range(2)]
                Vt = [sb1("Vt%d" % i, [128, c.NB, 128], BF16) for i in range(2)]
                qT = [sb1("qT%d" % i, [128, SOWN], BF16) for i in range(2)]
                e_t = [sb1("e_t%d" % i, [128, 2, 512], F32) for i in range(2)]
                sp_t = [sb1("sp_t%d" % i, [128, 2, 512], BF16) for i in range(3)]
                tt_t = [sb1("tt_t%d" % i, [128, 2, 512], F32) for i in range(3)]
                w_t = [sb1("w_t%d" % i, [128, 2, 512], BF16) for i in range(3)]
                carry = [sb1("carry%d" % i, [128, 512], F32) for i in range(2)]
                pt = [sb1("pt%d" % i, [128, 640], BF16) for i in range(3)]
                rden = [sb1("rden%d" % i, [128, 512], F32) for i in range(2)]
                bm_f = [sb1("bm_f%d" % i, [128, 640], F32) for i in range(2)]
                mk_f = sb1("mk_f", [128, 5 * 128], F32)

                dma("sp", mk_f[:], maskT_d, [], ["mk_f"])
                for h in range(NH):
                    bb = nxt("bmf", 2)
                    dma("sp", bm_f[bb][:], biasT_d[:, h * 640:(h + 1) * 640], [], [("bm_f", bb)])
                    tt("dve", bm_b[:, h * 640:(h + 1) * 640], bm_f[bb][:], mk_f[:], ALU.add, [("bm_f", bb), "mk_f"], [("bm_b", h)])

                def load_head(kd, vd, qd, h):
                    b = nxt("hbuf", 2)
                    dma("sp", kT[b][:], kd[h], [], [("kT", b)])
                    dma("sp", Vt[b][:], vd[:, h * 128:(h + 1) * 128].rearrange("(b p) n -> p b n", p=128), [], [("Vt", b)])
                    dma("sp", qT[b][:], qd[h], [], [("qT", b)])
                    return b

                N = c.QB * 128
                hbuf = {}
                tasks = []
                for h in range(NH):
                    for qg in range(c.NQG):
                        npair = c.QB * (qg + 1)
                        for pi, i in enumerate(range(npair - 1, -1, -1)):
                            tasks.append(dict(h=h, qg=qg, pi=pi, np=npair, i=i, first_of_head=(qg == 0 and pi == 0)))
                qgc = [0]
                pp3 = [pp[i][:].rearrange("p (a n) -> p a n", a=2) for i in range(4)]

                def geom(T):
                    i, qg = T["i"], T["qg"]
                    lo = max(0, i - c.QB * qg) * 128
                    diag = i >= c.QB * qg
                    b = hbuf[T["h"]]
                    k_own = kT[b][:, i * 128:(i + 1) * 128]
                    k_oth = kT[b][:, (NOWN + i) * 128:(NOWN + i + 1) * 128]
                    qr = qT[b][:, qg * N + lo: qg * N + N]
                    return lo, diag, b, k_own, k_oth, qr

                def S0(n):
                    T = tasks[n]
                    if T["first_of_head"] and T["h"] == 0:
                        hbuf[0] = load_head(kTsb_d, vsb_d, qTsb_d, 0)
                    if T["qg"] == 0 and T["pi"] == 3 and T["h"] + 1 < NH:
                        hbuf[T["h"] + 1] = load_head(kTsb_d, vsb_d, qTsb_d, T["h"] + 1)
                    lo, diag, b, k_own, k_oth, qr = geom(T)
                    a_ = n % 2
                    mm(ps[2 * a_][:, lo:N], k_own, qr, True, True, [("kT", b), ("qT", b)], [PS(2 * a_)])
                    mm(ps[2 * a_ + 1][:, lo:N], k_oth, qr, True, True, [("kT", b), ("qT", b)], [PS(2 * a_ + 1)])

                def S1(n):
                    T = tasks[n]
                    lo, diag, b, k_own, k_oth, qr = geom(T)
                    a_, r3 = n % 2, n % 3
                    act(e_t[a_][:, :, lo:N], pp3[a_][:, :, lo:N], AF.Exp, [PS(2 * a_), PS(2 * a_ + 1)], [("e_t", a_)])
                    act(sp_t[r3][:, :, lo:N], e_t[a_][:, :, lo:N], AF.Ln, [("e_t", a_)], [("sp_t", r3)], bias=1.0)
                    if diag:
                        tt("pool", sp_t[r3][:, 0, lo:lo + 128], sp_t[r3][:, 0, lo:lo + 128], tri_b, ALU.mult,
                           [("sp_t", r3), "cst_b"], [("sp_t", r3)])

                def S2(n):
                    T = tasks[n]
                    lo, diag, b, k_own, k_oth, qr = geom(T)
                    r3 = n % 3
                    if T["pi"] == 0:
                        qgc[0] += 1
                        T["cb"] = qgc[0] % 2
                        sc.op("pool", "memset", dict(ap=carry[T["cb"]][:, 0:N], constant=0.0), [], [("carry", T["cb"])])
                    else:
                        T["cb"] = tasks[n - 1]["cb"]
                    cb = T["cb"]
                    sp_own, sp_oth = sp_t[r3][:, 0, lo:N], sp_t[r3][:, 1, lo:N]
                    mm(ps[4][:, lo:N], k_own, qr, True, False, [("kT", b), ("qT", b)], [PS(4)])
                    mm(ps[4][:, lo:N], negU_b, sp_own, False, True, [("sp_t", r3), "cst_b"], [PS(4)])
                    mm(ps[5][:, lo:N], k_oth, qr, True, False, [("kT", b), ("qT", b)], [PS(5)])
                    mm(ps[5][:, lo:N], negU_b, sp_oth, False, False, [("sp_t", r3), "cst_b"], [PS(5)])
                    mm(ps[5][:, lo:N], nones_b, sp_own, False, True, [("sp_t", r3), "cst_b"], [PS(5)])
                    mm(ps[6][:, lo:N], ones_b, sp_own, True, False, [("sp_t", r3), "cst_b"], [PS(6)])
                    mm(ps[6][:, lo:N], ones_b, sp_oth, False, True, [("sp_t", r3), "cst_b"], [PS(6)])
                    tt("dve", tt_t[r3][:, 0, lo:N], ps[4][:, lo:N], carry[cb][:, lo:N], ALU.subtract, [PS(4), ("carry", cb)], [("tt_t", r3, 0)])
                    tt("dve", tt_t[r3][:, 1, lo:N], ps[5][:, lo:N], carry[cb][:, lo:N], ALU.subtract, [PS(5), ("carry", cb)], [("tt_t", r3, 1)])
                    tt("dve", carry[cb][:, lo:N], carry[cb][:, lo:N], ps[6][:, lo:N], ALU.add, [PS(6), ("carry", cb)], [("carry", cb)])

                def S3(n):
                    T = tasks[n]
                    lo, diag, b, k_own, k_oth, qr = geom(T)
                    r3 = n % 3
                    i = T["i"]
                    if lo > 0:
                        sc.op("pool", "memset", dict(ap=w_t[r3][:, :, 0:lo], constant=0.0), [], [("w_t", r3)])
                    act(w_t[r3][:, :, lo:N], tt_t[r3][:, :, lo:N], AF.Exp, [("tt_t", r3, 0), ("tt_t", r3, 1)], [("w_t", r3)])
                    if diag:
                        tt("pool", w_t[r3][:, 0, lo:lo + 128], w_t[r3][:, 0, lo:lo + 128], tri_b, ALU.mult,
                           [("w_t", r3), "cst_b"], [("w_t", r3)])
                    first, last = T["pi"] == 0, T["pi"] == T["np"] - 1
                    mm(ps[7][:, 0:N], Vt[b][:, i, :], w_t[r3][:, 0, 0:N], first, False, [("Vt", b), ("w_t", r3)], [PS(7)])
                    mm(ps[7][:, 0:N], Vt[b][:, NOWN + i, :], w_t[r3][:, 1, 0:N], False, last, [("Vt", b), ("w_t", r3)], [PS(7)])
                    if last:
                        h, qg = T["h"], T["qg"]
                        sc.op("dve", "tensor_copy", dict(out=ysb[:, h, qg * N:(qg + 1) * N], in_=ps[7][:, 0:N]), [PS(7)], [("ysb", h)])

                NTk = len(tasks)
                S0(0)
                for step in range(NTk + 2):
                    if step + 1 < NTk:
                        S0(step + 1)
                    if step < NTk:
                        S1(step)
                    if 1 <= step <= NTk:
                        S2(step - 1)
                    if step >= 2:
                        S3(step - 2)

                cbuf = {}
                ctasks = [(h, j) for h in range(NH) for j in range(NOWN)]
                cinfo = {}

                def C1(n):
                    h, j = ctasks[n]
                    if j == 0 and h == 0:
                        cbuf[0] = load_head(kTca_d, vca_d, qTca_d, 0)
                    if j == 2 and h + 1 < NH:
                        cbuf[h + 1] = load_head(kTca_d, vca_d, qTca_d, h + 1)
                    b = cbuf[h]
                    wl = [(0, "own", j - 2), (1, "oth", j - 1), (2, "own", j - 1), (3, "oth", j), (4, "own", j)]
                    wl = [x for x in wl if x[2] >= 0]
                    r = n % 2
                    pS0, pS1 = 0 + 2 * r, 1 + 2 * r
                    qblk = qT[b][:, j * 128:(j + 1) * 128]
                    for (wi, kind, i) in wl:
                        kslot = i if kind == "own" else NOWN + i
                        dstp = ps[pS0][:, wi * 128:(wi + 1) * 128] if wi < 4 else ps[pS1][:, 0:128]
                        dkey = PS(pS0) if wi < 4 else PS(pS1)
                        padb = (kind == "oth" and i == 0)
                        mm(dstp, kT[b][:, kslot * 128:(kslot + 1) * 128], qblk, True, False, [("kT", b), ("qT", b)], [dkey])
                        mm(dstp, ident_b, bm_b[:, (h * 5 + wi) * 128:(h * 5 + wi + 1) * 128], False, not padb,
                           ["cst_b", ("bm_b", h)], [dkey])
                        if padb:
                            mm(dstp, ident_b, pad_b[:], False, True, ["cst_b", "pad_b"], [dkey])
                    cinfo[n] = (wl, pS0, pS1, b)

                def C2(n):
                    h, j = ctasks[n]
                    wl, pS0, pS1, b = cinfo[n]
                    pr = n % 3
                    jj = j % 4
                    j0 = j - jj
                    nj = min(4, NOWN - j0)
                    if jj == 0:
                        cinfo[("pyd", h, j0)] = nxt("car", 2)
                    r2 = cinfo[("pyd", h, j0)]
                    py, pd = 4 + r2, 6 + r2
                    w0 = wl[0][0]
                    if w0 < 4:
                        act(pt[pr][:, w0 * 128:512], ps[pS0][:, w0 * 128:512], AF.Exp, [PS(pS0)], [("pt", pr)])
                    act(pt[pr][:, 512:640], ps[pS1][:, 0:128], AF.Exp, [PS(pS1)], [("pt", pr)])
                    for n_, (wi, kind, i) in enumerate(wl):
                        kslot = i if kind == "own" else NOWN + i
                        mm(ps[py][:, jj * 128:(jj + 1) * 128], Vt[b][:, kslot, :], pt[pr][:, wi * 128:(wi + 1) * 128],
                           n_ == 0, n_ == len(wl) - 1, [("Vt", b), ("pt", pr)], [PS(py)])
                    for n_, (wi, kind, i) in enumerate(wl):
                        mm(ps[pd][:, jj * 128:(jj + 1) * 128], ones_b, pt[pr][:, wi * 128:(wi + 1) * 128],
                           n_ == 0, n_ == len(wl) - 1, ["cst_b", ("pt", pr)], [PS(pd)])
                    if jj == nj - 1:
                        W = nj * 128
                        rd = nxt("rden", 2)
                        sc.op("dve", "reciprocal", dict(out=rden[rd][:, 0:W], in_=ps[pd][:, 0:W]), [PS(pd)], [("rden", rd)])
                        tt("dve", yca[:, h, j0 * 128: j0 * 128 + W], ps[py][:, 0:W], rden[rd][:, 0:W], ALU.mult,
                           [PS(py), ("rden", rd)], [("yca", h)])

                NC_ = len(ctasks)
                for step in range(NC_ + 1):
                    if step < NC_:
                        C1(step)
                    if step >= 1:
                        C2(step - 1)
                if debug:
                    for h in range(NH):
                        dma("sp", ysb_d[h], ysb[:, h, :], [("ysb", h)], ["dbg"])
                        dma("sp", yca_d[h], yca[:, h, :], [("yca", h)], ["dbg"])
                sc.barrier()
                sc.emit()

            woa = sbb("woa", [128, NH, D], BF16)
            wob = sbb("wob", [128, NH, D], BF16)
            gA = [sbb("gA%d" % i, [128, 512], BF16) for i in range(4)]
            gB = [sbb("gB%d" % i, [128, 512], BF16) for i in range(4)]
            m1 = [sbb("m1%d" % i, [128, 512], F32) for i in range(4)]
            m2 = [sbb("m2%d" % i, [128, 512], F32) for i in range(4)]
            mo = [sbb("mo%d" % i, [128, 512], BF16) for i in range(4)]
            NQW = 4 if KC % 4 == 0 else 1
            QW = D // NQW
            for q in range(NQW):
                dma("pool", woa[:, :, q * QW:(q + 1) * QW], w_o_sb[:, q * QW:(q + 1) * QW].rearrange("(h p) n -> p h n", p=128), [], [("woa", q)])
                dma("pool", wob[:, :, q * QW:(q + 1) * QW], w_o_ca[:, q * QW:(q + 1) * QW].rearrange("(h p) n -> p h n", p=128), [], [("wob", q)])
            c1steps = [(m, tg) for m in range(KC) for tg in range(SOWN // 512)]
            c1buf = {}

            def c1_loads(si):
                m, tg = c1steps[si]
                sl = slice(tg * 512, (tg + 1) * 512)
                r = nxt("c1r", 4)
                c1buf[si] = r
                dma("sp", gA[r][:], gates_d[m, :, sl], [], [("gA", r)])
                dma("sp", gB[r][:], gates_d[KC + m, :, sl], [], [("gB", r)])

            for si in range(min(3, len(c1steps))):
                c1_loads(si)
            for si, (m, tg) in enumerate(c1steps):
                if si + 3 < len(c1steps):
                    c1_loads(si + 3)
                wq = (m * 128) // QW
                sl = slice(tg * 512, (tg + 1) * 512)
                r = c1buf[si]
                pa, pb2 = 0 + r, 4 + r
                for h in range(NH):
                    mm(ps[pa][:], woa[:, h, m * 128:(m + 1) * 128], ysb[:, h, sl], h == 0, h == NH - 1, [("woa", wq)], [PS(pa)])
                for h in range(NH):
                    mm(ps[pb2][:], wob[:, h, m * 128:(m + 1) * 128], yca[:, h, sl], h == 0, h == NH - 1, [("wob", wq)], [PS(pb2)])
                tt("dve", m1[r][:], ps[pa][:], gA[r][:], ALU.mult, [PS(pa), ("gA", r)], [("m1", r)])
                tt("dve", m2[r][:], ps[pb2][:], gB[r][:], ALU.mult, [PS(pb2), ("gB", r)], [("m2", r)])
                tt("pool", mo[r][:], m1[r][:], m2[r][:], ALU.add, [("m1", r), ("m2", r)], [("mo", r)])
                dma("sp", mrg_d[m, :, sl], mo[r][:], [("mo", r)], ["dram_mrg"])
            sc.barrier()
            sc.emit()

        with contextlib.ExitStack() as phC:
            B = tile_bufs(phC, "_c")
            xT, xn, wg, xio = B["xT"], B["xn"], B["wg"], B["xio"]
            for t in range(c.NT_OWN):
                s0 = t * TT
                CH = 4 if KC % 4 == 0 else KC

                def load_mrg(s0_):
                    for k0 in range(0, KC, CH):
                        dma("sp", xn[:, k0:k0 + CH, :], mrg_d[k0:k0 + CH, :, s0_:s0_ + TT].rearrange("k p s -> p k s"), [],
                            [("xn", k, tg) for k in range(k0, k0 + CH) for tg in range(NTG)])

                if t == 0:
                    load_mrg(s0)
                for k0 in range(0, KC, CH):
                    dma("sp", xT[:, k0:k0 + CH, :], x1T_d[k0:k0 + CH, :, s0:s0 + TT].rearrange("k p s -> p k s"), [],
                        [("xT", k, tg) for k in range(k0, k0 + CH) for tg in range(NTG)])
                for gi in range(D // 256):
                    wt_, wkey = wload(B, w_out, gi * 256)
                    for cc in range(2):
                        m = gi * 2 + cc
                        for tg in range(NTG):
                            sl = slice(tg * 512, (tg + 1) * 512)
                            pb = 2 + nxt("fmps", 4)
                            for k in range(KC):
                                mm(ps[pb][:], wt_[:, k, cc * 128:(cc + 1) * 128], xn[:, k, sl], k == 0, k == KC - 1,
                                   [wkey, ("xn", k, tg)], [PS(pb)])
                            tt("dve", xT[:, m, sl], ps[pb][:], xT[:, m, sl], ALU.add, [PS(pb), ("xT", m, tg)], [("xT", m, tg)])
                rmsnorm(B, 2, True)
                ffn(B, w["ffn2_g"], w["ffn2_u"], w["ffn2_d"])
                if t + 1 < c.NT_OWN:
                    load_mrg(s0 + TT)
                rmsnorm(B, 3, False)
                for blk in range(TT // 128):
                    b = nxt("xio", 2)
                    tgk = (blk * 128) // 512
                    for q0 in range(0, KC, 4):
                        n = min(4, KC - q0)
                        pb = nxt("trps", 2)
                        for i in range(n):
                            sc.op("pe", "transpose", dict(out=ps[pb][:, i * 128:(i + 1) * 128],
                                                          in_=xT[:, q0 + i, blk * 128:(blk + 1) * 128], identity=ident_f),
                                  [("xT", q0 + i, tgk), "cst_f"], [PS(pb)])
                        if nxt("trev", 2) == 0:
                            act(xio[b][:, q0 * 128:(q0 + n) * 128], ps[pb][:, 0:n * 128], AF.Copy, [PS(pb)], [("xio", b)])
                        else:
                            sc.op("dve", "tensor_copy", dict(out=xio[b][:, q0 * 128:(q0 + n) * 128], in_=ps[pb][:, 0:n * 128]),
                                  [PS(pb)], [("xio", b)])
                    r0 = s0 + blk * 128
                    dma("sp", out_d[r0:r0 + 128, :], xio[b][:], [("xio", b)], ["dram_out"])
            sc.barrier()
            sc.emit()
    return nc


def host_consts():
    ident = np.eye(128, dtype=np.float32)
    ones = np.ones((128, 128), np.float32)
    j = np.arange(128)[:, None]
    s = np.arange(128)[None, :]
    negU = np.where(j >= s, -1.0, 0.0).astype(np.float32)
    tri = np.where(j < s, 1.0, 0.0).astype(np.float32)
    return np.concatenate([ident, ones, negU, tri, -ones], axis=1)


def ca_tables():
    wi = np.arange(5)[:, None, None]
    k = np.arange(128)[None, :, None]
    q = np.arange(128)[None, None, :]
    dist = 128 * (4 - wi) + q - k
    ridx = np.clip(dist, -63, 128) + 63
    dchunk = 2 * (4 - wi) + (q >= 64) - (k >= 64)
    valid = (dchunk >= 0) & (dchunk <= 8)
    mask = np.where(valid, 0.0, NEG).astype(np.float32)
    return ridx, mask


def make_in_maps(cfg, inputs):
    c = cfg
    D, KC, NH = c.D, c.KC, c.NH
    f = lambda a: np.ascontiguousarray(np.asarray(a, dtype=np.float32))
    x = f(inputs["x"])
    gains = np.concatenate([f(inputs[n])[0].reshape(KC, 128).T for n in ("ffn1_norm", "mix_norm", "ffn2_norm", "final_norm")], axis=1)
    bgate = f(inputs["b_gate"])[0].reshape(2 * KC, 128).T
    qkg = np.stack([f(inputs["q_norm_ca"])[0], f(inputs["k_norm_ca"])[0]], axis=1)
    ridx, mask = ca_tables()
    rb = f(inputs["rel_bias"])[0]
    biasT = rb[:, ridx]
    biasT = np.ascontiguousarray(biasT.transpose(2, 0, 1, 3)).reshape(128, NH * 5 * 128)
    maskT = np.ascontiguousarray(mask.transpose(1, 0, 2)).reshape(128, 5 * 128)
    consts = host_consts()
    shared = {
        "gains": np.ascontiguousarray(gains), "bgate": np.ascontiguousarray(bgate), "qkg": np.ascontiguousarray(qkg),
        "biasT": biasT, "maskT": maskT, "consts": consts,
        "w_in": f(inputs["w_in"])[0], "w_o_sb": f(inputs["w_o_sb"])[0], "w_o_ca": f(inputs["w_o_ca"])[0], "w_out": f(inputs["w_out"])[0],
    }
    for pre in ("ffn1", "ffn2"):
        for n in ("w_gate", "w_up", "w_down"):
            shared[pre + "_" + n] = f(inputs[pre + "_" + n])[0]
    maps = []
    for core in range(8):
        b, p = core // 2, core % 2
        xb = x[b].reshape(c.NB, 128, D)
        own = xb[p::2]
        if p == 1:
            oth = xb[0::2]
        else:
            oth = np.concatenate([np.zeros((1, 128, D), np.float32), xb[1::2][:-1]], axis=0)
        xall = np.ascontiguousarray(np.concatenate([own, oth], axis=0).reshape(c.S, D))
        pad = np.full((128, 128), NEG if p == 0 else 0.0, np.float32)
        m = dict(shared)
        m["xall"] = xall
        m["padmask"] = pad
        maps.append(m)
    return maps


def assemble(cfg, results):
    c = cfg
    out = np.zeros((4, c.S, c.D), np.float32)
    for core in range(8):
        b, p = core // 2, core % 2
        o = np.asarray(results[core]["out"], dtype=np.float32).reshape(c.NOWN, 128, c.D)
        out[b].reshape(c.NB, 128, c.D)[p::2] = o
    return out


_NC_CACHE = {}


def kernel(**inputs):
    cfg = Cfg()
    if "nc" not in _NC_CACHE:
        _NC_CACHE["nc"] = build(cfg)
    nc = _NC_CACHE["nc"]
    maps = make_in_maps(cfg, inputs)
    res = run_bass_kernel_spmd(nc, maps, core_ids=list(range(8)))
    return assemble(cfg, res.results)
```

```python
import numpy as np
import ml_dtypes
import concourse.bass as bass
import concourse.mybir as mybir
from concourse.bass_utils import run_bass_kernel_spmd

F32 = mybir.dt.float32
BF16 = mybir.dt.bfloat16
AF = mybir.ActivationFunctionType
ALU = mybir.AluOpType
NEG = -30000.0


class Cfg:
    def __init__(self, D=2048, DFF=5632, S=4096, NH=8, TT=1024, EPS=1e-6):
        self.D, self.DFF, self.S, self.NH, self.TT, self.EPS = D, DFF, S, NH, TT, EPS
        self.KC = D // 128
        self.NB = S // 128
        self.NOWN = self.NB // 2
        self.SOWN = S // 2
        self.NT = S // TT
        self.NT_OWN = self.SOWN // TT
        self.HW = NH * 128
        self.INC = 6 * self.HW + 2 * D
        self.GF = DFF // 256
        self.QB = min(4, self.NOWN)
        self.NQG = self.NOWN // self.QB
        self.NTG = TT // 512


class Sched:
    ENGS = ["pe", "act", "dve", "pool", "sp"]

    def __init__(self, nc, stack, n_dma_sems=8):
        self.nc = nc
        self.ops = []
        self.emitted = 0
        self.lastw = {}
        self.readers = {}
        self.n_dma_sems = n_dma_sems
        self.dma_rr = {e: 0 for e in self.ENGS}
        self.dma_cnt = {}
        self.dma_last = {}
        self.last_on_eng = {}
        self.cnt = {e: 0 for e in self.ENGS}
        self.seen = {e: {} for e in self.ENGS}
        self.csem = {e: stack.enter_context(nc.semaphore("c_" + e)) for e in self.ENGS}
        self.dsem = {}
        for e in ("sp", "pool"):
            for i in range(n_dma_sems):
                self.dsem[(e, i)] = stack.enter_context(nc.semaphore("d_%s%d" % (e, i)))

    def op(self, eng, meth, kw, reads=(), writes=(), dma=False):
        oid = len(self.ops)
        deps = set()
        for k in reads:
            w = self.lastw.get(k)
            if w is not None:
                deps.add(w)
        for k in writes:
            w = self.lastw.get(k)
            if w is not None:
                deps.add(w)
            for r in self.readers.get(k, ()):
                deps.add(r)
        for k in writes:
            self.lastw[k] = oid
            self.readers[k] = []
        for k in reads:
            self.readers.setdefault(k, []).append(oid)
        deps.discard(oid)
        o = dict(id=oid, eng=eng, meth=meth, kw=kw, deps=deps, dma=dma, sig=None, needed=False)
        if dma:
            i = self.dma_rr[eng]
            self.dma_rr[eng] = (i + 1) % self.n_dma_sems
            key = (eng, i)
            prev = self.dma_last.get(key)
            if prev is not None:
                deps.add(prev)
            self.dma_cnt[key] = self.dma_cnt.get(key, 0) + 1
            o["dsem"] = key
            o["dval"] = 16 * self.dma_cnt[key]
            self.dma_last[key] = oid
        self.ops.append(o)
        self.last_on_eng[eng] = oid
        return o

    def barrier(self):
        deps = set(v for v in self.last_on_eng.values() if v >= self.emitted) | set(self.dma_last.values())
        self.lastw = {}
        self.readers = {}
        b0 = self.op("sp", "nop", {}, writes=["__bar__"])
        b0["deps"] |= deps
        b0["deps"].discard(b0["id"])
        for e in self.ENGS:
            if e != "sp":
                self.op(e, "nop", {}, reads=["__bar__"])
        self.lastw = {}
        self.readers = {}

    def emit(self):
        nc = self.nc
        ops = self.ops
        new = ops[self.emitted:]
        self.emitted = len(ops)

        def pe_pe(p, o):
            return p["eng"] == "pe" and o["eng"] == "pe" and not o["dma"] and not p["dma"]

        for o in new:
            for d in o["deps"]:
                p = ops[d]
                if p["dma"] or pe_pe(p, o):
                    continue
                assert p["sig"] is not None or d >= len(ops) - len(new), "dep on already-emitted unsignalled op"
                p["needed"] = True
        for o in new:
            if (not o["dma"]) and o["needed"]:
                self.cnt[o["eng"]] += 1
                o["sig"] = self.cnt[o["eng"]]
        by_eng = {e: [o for o in new if o["eng"] == e] for e in self.ENGS}
        csem, dsem = self.csem, self.dsem

        def run(ename, eng):
            seen = self.seen[ename]
            for o in by_eng[ename]:
                need = {}
                for d in o["deps"]:
                    p = ops[d]
                    if p["dma"]:
                        s, v = dsem[p["dsem"]], p["dval"]
                    else:
                        if pe_pe(p, o):
                            continue
                        s, v = csem[p["eng"]], p["sig"]
                    if v > need.get(s, 0):
                        need[s] = v
                for s, v in need.items():
                    if v > seen.get(s, 0):
                        eng.wait_ge(s, v)
                        seen[s] = v
                ins = getattr(eng, o["meth"])(**o["kw"])
                if o["dma"]:
                    ins.then_inc(dsem[o["dsem"]], 16)
                elif o["sig"] is not None:
                    ins.then_inc(csem[ename], 1)

        with nc.Block() as block:
            block.tensor(lambda e: run("pe", e))
            block.scalar(lambda e: run("act", e))
            block.vector(lambda e: run("dve", e))
            block.gpsimd(lambda e: run("pool", e))
            block.sync(lambda e: run("sp", e))


def build(cfg, debug=False):
    import contextlib
    c = cfg
    D, DFF, S, NH, TT, KC = c.D, c.DFF, c.S, c.NH, c.TT, c.KC
    NOWN, SOWN, HW, NTG = c.NOWN, c.SOWN, c.HW, c.NTG
    nc = bass.Bass("TRN2", target_bir_lowering=False)

    def din(name, shape, dt=F32):
        return nc.dram_tensor(name, list(shape), dt, kind="ExternalInput").ap()

    scratch_kind = "ExternalOutput" if debug else "Internal"

    def dsc(name, shape, dt=BF16):
        return nc.dram_tensor(name, list(shape), dt, kind=scratch_kind).ap()

    xall = din("xall", [S, D])
    w = {}
    for pre in ("ffn1", "ffn2"):
        w[pre + "_g"] = din(pre + "_w_gate", [D, DFF])
        w[pre + "_u"] = din(pre + "_w_up", [D, DFF])
        w[pre + "_d"] = din(pre + "_w_down", [DFF, D])
    w_in = din("w_in", [D, c.INC])
    w_o_sb = din("w_o_sb", [HW, D])
    w_o_ca = din("w_o_ca", [HW, D])
    w_out = din("w_out", [D, D])
    gains_d = din("gains", [128, 4 * KC])
    bgate_d = din("bgate", [128, 2 * KC])
    qkg_d = din("qkg", [128, 2])
    biasT_d = din("biasT", [128, NH * 5 * 128])
    maskT_d = din("maskT", [128, 5 * 128])
    pad_d = din("padmask", [128, 128])
    cst_d = din("consts", [128, 5 * 128])
    out_d = nc.dram_tensor("out", [SOWN, D], F32, kind="ExternalOutput").ap()

    kTsb_d = dsc("kTsb", [NH, 128, S])
    kTca_d = dsc("kTca", [NH, 128, S])
    vsb_d = dsc("vsb", [S, HW])
    vca_d = dsc("vca", [S, HW])
    qTsb_d = dsc("qTsb", [NH, 128, SOWN])
    qTca_d = dsc("qTca", [NH, 128, SOWN])
    gates_d = dsc("gates", [2 * KC, 128, SOWN])
    x1T_d = dsc("x1T", [KC, 128, SOWN], F32)
    mrg_d = dsc("mrg", [KC, 128, SOWN])
    ysb_d = dsc("ysbd", [NH, 128, SOWN]) if debug else None
    yca_d = dsc("ycad", [NH, 128, SOWN]) if debug else None

    with contextlib.ExitStack() as top:
        sc = Sched(nc, top)

        def sb(name, shape, dt):
            return top.enter_context(nc.sbuf_tensor(name, list(shape), dt))

        pp = [top.enter_context(nc.psum_tensor("pp%d" % i, [128, 1024], F32)) for i in range(4)]
        ps = [pp[i // 2][:, (i % 2) * 512:(i % 2 + 1) * 512] for i in range(8)]
        PS = lambda i: ("ps", i)

        rot = {}

        def nxt(name, n):
            v = rot.get(name, 0)
            rot[name] = (v + 1) % n
            return v

        def mm(out, lhsT, rhs, start, stop, reads, writes):
            sc.op("pe", "matmul", dict(out=out, lhsT=lhsT, rhs=rhs, start=start, stop=stop), reads, writes)

        def act(out, in_, func, reads, writes, **kw):
            sc.op("act", "activation", dict(out=out, in_=in_, func=func, **kw), reads, writes)

        def tt(eng, out, in0, in1, op, reads, writes):
            sc.op(eng, "tensor_tensor", dict(out=out, in0=in0, in1=in1, op=op), reads, writes)

        def stt(out, in0, scalar, in1, op0, op1, reads, writes):
            sc.op("dve", "scalar_tensor_tensor", dict(out=out, in0=in0, scalar=scalar, in1=in1, op0=op0, op1=op1), reads, writes)

        def dma(q, out, in_, reads, writes):
            sc.op(q, "dma_start", dict(out=out, in_=in_), reads, writes, dma=True)

        cst_f = sb("cst_f", [128, 5 * 128], F32)
        ident_f = cst_f[:, 0:128]
        cst_b = sb("cst_b", [128, 5 * 128], BF16)
        ident_b, ones_b, negU_b, tri_b, nones_b = (cst_b[:, i * 128:(i + 1) * 128] for i in range(5))
        gains = sb("gains_sb", [128, 4 * KC], F32)
        bgate = sb("bgate_sb", [128, 2 * KC], F32)
        qkg = sb("qkg_sb", [128, 2], F32)
        qkg_s = sb("qkg_s", [128, 2], F32)
        epsD = sb("epsD", [128, 1], F32)
        one_t = sb("one_t", [128, 1], F32)
        pad_b = sb("pad_b", [128, 128], BF16)
        bm_b = sb("bm_b", [128, NH * 5 * 128], BF16)

        dma("sp", cst_f[:], cst_d, [], ["cst_f"])
        dma("sp", gains[:], gains_d, [], ["gains"])
        dma("sp", bgate[:], bgate_d, [], ["bgate"])
        dma("sp", qkg[:], qkg_d, [], ["qkg"])
        dma("pool", pad_b[:], pad_d, [], ["pad_b"])
        sc.op("dve", "tensor_copy", dict(out=cst_b[:], in_=cst_f[:]), ["cst_f"], ["cst_b"])
        sc.op("dve", "memset", dict(ap=epsD[:], constant=c.EPS), [], ["epsD"])
        sc.op("dve", "memset", dict(ap=one_t[:], constant=1.0), [], ["one_t"])
        sc.op("dve", "tensor_scalar", dict(out=qkg_s[:, 0:1], in0=qkg[:, 0:1], scalar1=128.0 ** -0.5, scalar2=None, op0=ALU.mult),
              ["qkg"], ["qkg_s0"])
        sc.op("dve", "tensor_copy", dict(out=qkg_s[:, 1:2], in_=qkg[:, 1:2]), ["qkg"], ["qkg_s1"])

        def tile_bufs(stack, sfx):
            def sa(name, shape, dt):
                return stack.enter_context(nc.sbuf_tensor(name + sfx, list(shape), dt))
            B = {}
            B["xT"] = sa("xT", [128, KC, TT], F32)
            B["xn"] = sa("xn", [128, KC, TT], BF16)
            B["wg"] = [sa("wg%d" % i, [128, KC, 256], BF16) for i in range(2)]
            B["wu"] = [sa("wu%d" % i, [128, KC, 256], BF16) for i in range(2)]
            B["wd"] = [sa("wd%d" % i, [128, 2, D], BF16) for i in range(2)]
            B["hb"] = [sa("hb%d" % i, [128, 2, TT], BF16) for i in range(2)]
            B["xio"] = [sa("xio%d" % i, [128, D], F32) for i in range(2)]
            B["sq"] = [sa("sq%d" % i, [128, 512], BF16) for i in range(2)]
            B["rs"] = sa("rs", [128, 512], F32)
            B["rsn"] = [sa("rsn%d" % i, [128, 512], F32) for i in range(2)]
            B["sil"] = [sa("sil%d" % i, [128, 512], F32) for i in range(2)]
            B["ost"] = [sa("ost%d" % i, [128, 512], BF16) for i in range(2)]
            B["vst"] = sa("vst", [128, TT // 128, 256], BF16)
            return B

        XT_ALL = [("xT", m, tg) for m in range(KC) for tg in range(NTG)]
        XN_ALL = [("xn", k, tg) for k in range(KC) for tg in range(NTG)]

        xbuf = {}

        def issue_x(B, t, blk):
            b = nxt("xio", 2)
            xbuf[(t, blk)] = b
            r0 = t * TT + blk * 128
            dma("sp", B["xio"][b][:], xall[r0:r0 + 128, :], [], [("xio", b)])

        def transpose_blk(B, t, blk):
            xT, xio = B["xT"], B["xio"]
            b = xbuf[(t, blk)]
            tgk = (blk * 128) // 512
            for q0 in range(0, KC, 4):
                n = min(4, KC - q0)
                pb = 6 + nxt("trps", 2)
                for i in range(n):
                    sc.op("pe", "transpose", dict(out=ps[pb][:, i * 128:(i + 1) * 128],
                                                  in_=xio[b][:, (q0 + i) * 128:(q0 + i + 1) * 128], identity=ident_f),
                          [("xio", b), "cst_f"], [PS(pb)])
                src = ps[pb][:, 0:n * 128].rearrange("p (a b) -> p a b", a=n)
                dst = xT[:, q0:q0 + n, blk * 128:(blk + 1) * 128]
                wr = [("xT", m, tgk) for m in range(q0, q0 + n)]
                if nxt("trev", 2) == 0:
                    act(dst, src, AF.Copy, [PS(pb)], wr)
                else:
                    sc.op("dve", "tensor_copy", dict(out=dst, in_=src), [PS(pb)], wr)

        def load_steps(B, t):
            nb_ = TT // 128

            def mk(k):
                def f():
                    transpose_blk(B, t, k)
                    if k + 2 < nb_:
                        issue_x(B, t, k + 2)
                return f
            return [mk(k) for k in range(nb_)]

        def load_prologue(B, t):
            issue_x(B, t, 0)
            if TT // 128 > 1:
                issue_x(B, t, 1)

        def load_transpose(B, t):
            load_prologue(B, t)
            for f in load_steps(B, t):
                f()

        def rmsnorm(B, which, to_xn):
            xT, xn, sq, rs = B["xT"], B["xn"], B["sq"], B["rs"]
            for tg in range(NTG):
                sl = slice(tg * 512, (tg + 1) * 512)
                for k in range(KC):
                    s_ = nxt("sq", 2)
                    act(sq[s_][:], xT[:, k, sl], AF.Square, [("xT", k, tg)], [("sq", s_)])
                    mm(ps[1][:], ones_b, sq[s_][:], k == 0, k == KC - 1, [("sq", s_), "cst_b"], [PS(1)])
                act(rs[:], ps[1][:], AF.Sqrt, [PS(1), "epsD"], ["rs"], scale=1.0 / D, bias=epsD[:, 0:1])
                sc.op("dve", "reciprocal", dict(out=rs[:], in_=rs[:]), ["rs"], ["rs"])
                for k in range(KC):
                    gcol = gains[:, which * KC + k: which * KC + k + 1]
                    if to_xn:
                        stt(xn[:, k, sl], xT[:, k, sl], gcol, rs[:], ALU.mult, ALU.mult, [("xT", k, tg), "rs", "gains"], [("xn", k, tg)])
                    else:
                        stt(xT[:, k, sl], xT[:, k, sl], gcol, rs[:], ALU.mult, ALU.mult, [("xT", k, tg), "rs", "gains"], [("xT", k, tg)])

        wcnt = {"g": 0}

        def ffn(B, Wg, Wu, Wd):
            xT, xn, wg, wu, wd, hb, sil = B["xT"], B["xn"], B["wg"], B["wu"], B["wd"], B["hb"], B["sil"]

            def down(g, slot, pairs):
                hp = g % 2
                for (m, tg) in pairs:
                    sl = slice(tg * 512, (tg + 1) * 512)
                    pb = 6 + nxt("dps", 2)
                    for cc in range(2):
                        mm(ps[pb][:], wd[slot][:, cc, m * 128:(m + 1) * 128], hb[hp][:, cc, sl], cc == 0, cc == 1,
                           [("wd", slot), ("hb", hp, cc, tg)], [PS(pb)])
                    stt(xT[:, m, sl], ps[pb][:], 0.5, xT[:, m, sl], ALU.mult, ALU.add, [PS(pb), ("xT", m, tg)], [("xT", m, tg)])

            all_pairs = [(m, tg) for m in range(KC) for tg in range(NTG)]
            combos = [(cc, tg) for cc in range(2) for tg in range(NTG)]
            npart = len(combos)
            parts = [all_pairs[i * len(all_pairs) // npart:(i + 1) * len(all_pairs) // npart] for i in range(npart)]
            prev = None
            for g in range(c.GF):
                slot = wcnt["g"] % 2
                wcnt["g"] += 1
                c0 = g * 256
                dma("pool", wg[slot][:], Wg[:, c0:c0 + 256].rearrange("(k p) n -> p k n", p=128), [], [("wg", slot)])
                dma("pool", wu[slot][:], Wu[:, c0:c0 + 256].rearrange("(k p) n -> p k n", p=128), [], [("wu", slot)])
                dma("pool", wd[slot][:], Wd[c0:c0 + 256, :].rearrange("(k p) n -> p k n", p=128), [], [("wd", slot)])
                for ci, (cc, tg) in enumerate(combos):
                    sl = slice(tg * 512, (tg + 1) * 512)
                    r = nxt("gups", 2)
                    pg, pu = 2 + r, 4 + r
                    dl = list(parts[ci]) if prev is not None else []
                    nmm = 2 * KC
                    every = max(1, nmm // max(1, len(dl))) if dl else 0
                    cnt = 0
                    for (pbk, wsrc, wkey) in ((pg, wg, "wg"), (pu, wu, "wu")):
                        for k in range(KC):
                            mm(ps[pbk][:], wsrc[slot][:, k, cc * 128:(cc + 1) * 128], xn[:, k, sl], k == 0, k == KC - 1,
                               [(wkey, slot), ("xn", k, tg)], [PS(pbk)])
                            cnt += 1
                            if dl and cnt % every == 0:
                                down(prev[0], prev[1], [dl.pop(0)])
                    act(sil[r][:], ps[pg][:], AF.Silu, [PS(pg)], [("sil", r)])
                    tt("dve", hb[g % 2][:, cc, sl], sil[r][:], ps[pu][:], ALU.mult, [("sil", r), PS(pu)], [("hb", g % 2, cc, tg)])
                    if dl:
                        down(prev[0], prev[1], dl)
                prev = (g, slot)
            down(prev[0], prev[1], [(m, tg) for tg in range(NTG) for m in range(KC)])

        def wload(B, Wap, col0):
            r = wcnt["r"] = (wcnt.get("r", -1) + 1) % 4
            name, slot = ("wg", r) if r < 2 else ("wu", r - 2)
            tile_ = B[name][slot]
            dma("pool", tile_[:], Wap[:, col0:col0 + 256].rearrange("(k p) n -> p k n", p=128), [], [(name, slot)])
            return tile_, (name, slot)

        pend = []

        def flush_pend():
            while pend:
                f, a = pend.pop(0)
                f(*a)

        def fm_group(B, Wap, col0, evac):
            xn = B["xn"]
            wt_, wkey = wload(B, Wap, col0)
            for cc in range(2):
                for tg in range(NTG):
                    sl = slice(tg * 512, (tg + 1) * 512)
                    pb = 2 + nxt("fmps", 4)
                    for k in range(KC):
                        mm(ps[pb][:], wt_[:, k, cc * 128:(cc + 1) * 128], xn[:, k, sl], k == 0, k == KC - 1,
                           [wkey, ("xn", k, tg)], [PS(pb)])
                    flush_pend()
                    pend.append((evac, (cc, tg, pb)))

        def win_phase(B, t, filler=None):
            xn, wg, sq, rs, ost, vst = B["xn"], B["wg"], B["sq"], B["rs"], B["ost"], B["vst"]
            own = t < c.NT_OWN
            s0 = t * TT

            def plain(dst, scale):
                def ev(gidx):
                    def f(cc, tg, pb):
                        head = gidx * 2 + cc
                        ob = nxt("ost", 2)
                        act(ost[ob][:], ps[pb][:], AF.Copy, [PS(pb)], [("ost", ob)], scale=scale)
                        dma("sp", dst[head, :, s0 + tg * 512: s0 + (tg + 1) * 512], ost[ob][:], [("ost", ob)], ["dram_qk"])
                    return f
                return ev

            def normed(dst, gi_):
                def ev(gidx):
                    def f(cc, tg, pb):
                        head = gidx * 2 + cc
                        s_ = nxt("sq", 2)
                        rb = nxt("rsn", 2)
                        rsb = B["rsn"][rb]
                        act(sq[s_][:], ps[pb][:], AF.Square, [PS(pb)], [("sq", s_)])
                        mm(ps[rb][:], ones_b, sq[s_][:], True, True, [("sq", s_), "cst_b"], [PS(rb)])
                        act(rsb[:], ps[rb][:], AF.Sqrt, [PS(rb), "epsD"], [("rsn", rb)], scale=1.0 / 128, bias=epsD[:, 0:1])
                        sc.op("dve", "reciprocal", dict(out=rsb[:], in_=rsb[:]), [("rsn", rb)], [("rsn", rb)])
                        ob = nxt("ost", 2)
                        stt(ost[ob][:], ps[pb][:], qkg_s[:, gi_:gi_ + 1], rsb[:], ALU.mult, ALU.mult,
                            [PS(pb), ("rsn", rb), "qkg_s0", "qkg_s1"], [("ost", ob)])
                        dma("sp", dst[head, :, s0 + tg * 512: s0 + (tg + 1) * 512], ost[ob][:], [("ost", ob)], ["dram_qk"])
                    return f
                return ev

            def gate_ev(gidx):
                def f(cc, tg, pb):
                    ch = gidx * 2 + cc
                    ob = nxt("ost", 2)
                    act(ost[ob][:], ps[pb][:], AF.Sigmoid, [PS(pb), "bgate"], [("ost", ob)], bias=bgate[:, ch:ch + 1])
                    dma("sp", gates_d[ch, :, s0 + tg * 512: s0 + (tg + 1) * 512], ost[ob][:], [("ost", ob)], ["dram_gates"])
                return f

            sections = []
            if own:
                sections.append((0, HW // 256, plain(qTsb_d, 128.0 ** -0.5)))
            sections.append((HW, HW // 256, plain(kTsb_d, 1.0)))
            if own:
                sections.append((3 * HW, HW // 256, normed(qTca_d, 0)))
            sections.append((4 * HW, HW // 256, normed(kTca_d, 1)))
            if own:
                sections.append((6 * HW, 2 * D // 256, gate_ev))
            ngroups = sum(x[1] for x in sections) + 2 * (HW // 256)
            fill_every = max(1, ngroups // (TT // 128 + 1))
            gcount = [0]

            def fill():
                gcount[0] += 1
                if filler and gcount[0] % fill_every == 0:
                    filler.pop(0)()

            for (cbase, ng, evf) in sections:
                for gi in range(ng):
                    fm_group(B, w_in, cbase + gi * 256, evf(gi))
                    fill()
            flush_pend()
            for (cbase, vd) in ((2 * HW, vsb_d), (5 * HW, vca_d)):
                for gi in range(HW // 256):
                    wt_, wkey = wload(B, w_in, cbase + gi * 256)
                    for tb in range(TT // 128):
                        pb = 2 + nxt("fmps", 4)
                        tgk = (tb * 128) // 512
                        for k in range(KC):
                            mm(ps[pb][:, 0:256], xn[:, k, tb * 128:(tb + 1) * 128], wt_[:, k, :], k == 0, k == KC - 1,
                               [wkey, ("xn", k, tgk)], [PS(pb)])
                        if nxt("vev", 2) == 0:
                            act(vst[:, tb, :], ps[pb][:, 0:256], AF.Copy, [PS(pb)], [("vst", tb)])
                        else:
                            sc.op("dve", "tensor_copy", dict(out=vst[:, tb, :], in_=ps[pb][:, 0:256]), [PS(pb)], [("vst", tb)])
                    dst = vd[s0:s0 + TT, gi * 256:(gi + 1) * 256].rearrange("(b p) n -> p b n", p=128)
                    dma("sp", dst, vst[:], [("vst", tb) for tb in range(TT // 128)], ["dram_v"])
                    fill()
            while filler:
                filler.pop(0)()

        with contextlib.ExitStack() as phA:
            B = tile_bufs(phA, "_a")
            load_transpose(B, 0)
            for t in range(c.NT):
                rmsnorm(B, 0, True)
                ffn(B, w["ffn1_g"], w["ffn1_u"], w["ffn1_d"])
                if t < c.NT_OWN:
                    dma("sp", x1T_d[:, :, t * TT:(t + 1) * TT].rearrange("k p s -> p k s"), B["xT"][:], XT_ALL, ["dram_x1"])
                rmsnorm(B, 1, True)
                filler = []
                if t + 1 < c.NT:
                    load_prologue(B, t + 1)
                    filler = load_steps(B, t + 1)
                win_phase(B, t, filler)
            sc.barrier()
            sc.emit()

        with contextlib.ExitStack() as phB:
            def sbb(name, shape, dt):
                return phB.enter_context(nc.sbuf_tensor(name, list(shape), dt))

            ysb = sbb("ysb", [128, NH, SOWN], BF16)
            yca = sbb("yca", [128, NH, SOWN], BF16)
            with contextlib.ExitStack() as phB1:
                def sb1(name, shape, dt):
                    return phB1.enter_context(nc.sbuf_tensor(name, list(shape), dt))

                kT = [sb1("kT%d" % i, [128, S], BF16) for i in range(2)]
                Vt = [sb1("Vt%d" % i, [128, c.NB, 128], BF16) for i in range(2)]
                qT = [sb1("qT%d" % i, [128, SOWN], BF16) for i in range(2)]
                e_t = [sb1("e_t%d" % i, [128, 2, 512], F32) for i in range(2)]
                sp_t = [sb1("sp_t%d" % i, [128, 2, 512], BF16) for i in range(3)]
                tt_t = [sb1("tt_t%d" % i, [128, 2, 512], F32) for i in range(3)]
                w_t = [sb1("w_t%d" % i, [128, 2, 512], BF16) for i in range(3)]
                carry = [sb1("carry%d" % i, [128, 512], F32) for i in range(2)]
                pt = [sb1("pt%d" % i, [128, 640], BF16) for i in range(3)]
                rden = [sb1("rden%d" % i, [128, 512], F32) for i in range(2)]
                bm_f = [sb1("bm_f%d" % i, [128, 640], F32) for i in range(2)]
                mk_f = sb1("mk_f", [128, 5 * 128], F32)

                dma("sp", mk_f[:], maskT_d, [], ["mk_f"])
                for h in range(NH):
                    bb = nxt("bmf", 2)
                    dma("sp", bm_f[bb][:], biasT_d[:, h * 640:(h + 1) * 640], [], [("bm_f", bb)])
                    tt("dve", bm_b[:, h * 640:(h + 1) * 640], bm_f[bb][:], mk_f[:], ALU.add, [("bm_f", bb), "mk_f"], [("bm_b", h)])

                def load_head(kd, vd, qd, h):
                    b = nxt("hbuf", 2)
                    dma("sp", kT[b][:], kd[h], [], [("kT", b)])
                    dma("sp", Vt[b][:], vd[:, h * 128:(h + 1) * 128].rearrange("(b p) n -> p b n", p=128), [], [("Vt", b)])
                    dma("sp", qT[b][:], qd[h], [], [("qT", b)])
                    return b

                N = c.QB * 128
                hbuf = {}
                tasks = []
                for h in range(NH):
                    for qg in range(c.NQG):
                        npair = c.QB * (qg + 1)
                        for pi, i in enumerate(range(npair - 1, -1, -1)):
                            tasks.append(dict(h=h, qg=qg, pi=pi, np=npair, i=i, first_of_head=(qg == 0 and pi == 0)))
                qgc = [0]
                pp3 = [pp[i][:].rearrange("p (a n) -> p a n", a=2) for i in range(4)]

                def geom(T):
                    i, qg = T["i"], T["qg"]
                    lo = max(0, i - c.QB * qg) * 128
                    diag = i >= c.QB * qg
                    b = hbuf[T["h"]]
                    k_own = kT[b][:, i * 128:(i + 1) * 128]
                    k_oth = kT[b][:, (NOWN + i) * 128:(NOWN + i + 1) * 128]
                    qr = qT[b][:, qg * N + lo: qg * N + N]
                    return lo, diag, b, k_own, k_oth, qr

                def S0(n):
                    T = tasks[n]
                    if T["first_of_head"] and T["h"] == 0:
                        hbuf[0] = load_head(kTsb_d, vsb_d, qTsb_d, 0)
                    if T["qg"] == 0 and T["pi"] == 3 and T["h"] + 1 < NH:
                        hbuf[T["h"] + 1] = load_head(kTsb_d, vsb_d, qTsb_d, T["h"] + 1)
                    lo, diag, b, k_own, k_oth, qr = geom(T)
                    a_ = n % 2
                    mm(ps[2 * a_][:, lo:N], k_own, qr, True, True, [("kT", b), ("qT", b)], [PS(2 * a_)])
                    mm(ps[2 * a_ + 1][:, lo:N], k_oth, qr, True, True, [("kT", b), ("qT", b)], [PS(2 * a_ + 1)])

                def S1(n):
                    T = tasks[n]
                    lo, diag, b, k_own, k_oth, qr = geom(T)
                    a_, r3 = n % 2, n % 3
                    act(e_t[a_][:, :, lo:N], pp3[a_][:, :, lo:N], AF.Exp, [PS(2 * a_), PS(2 * a_ + 1)], [("e_t", a_)])
                    act(sp_t[r3][:, :, lo:N], e_t[a_][:, :, lo:N], AF.Ln, [("e_t", a_)], [("sp_t", r3)], bias=1.0)
                    if diag:
                        tt("pool", sp_t[r3][:, 0, lo:lo + 128], sp_t[r3][:, 0, lo:lo + 128], tri_b, ALU.mult,
                           [("sp_t", r3), "cst_b"], [("sp_t", r3)])

                def S2(n):
                    T = tasks[n]
                    lo, diag, b, k_own, k_oth, qr = geom(T)
                    r3 = n % 3
                    if T["pi"] == 0:
                        qgc[0] += 1
                        T["cb"] = qgc[0] % 2
                        sc.op("pool", "memset", dict(ap=carry[T["cb"]][:, 0:N], constant=0.0), [], [("carry", T["cb"])])
                    else:
                        T["cb"] = tasks[n - 1]["cb"]
                    cb = T["cb"]
                    sp_own, sp_oth = sp_t[r3][:, 0, lo:N], sp_t[r3][:, 1, lo:N]
                    mm(ps[4][:, lo:N], k_own, qr, True, False, [("kT", b), ("qT", b)], [PS(4)])
                    mm(ps[4][:, lo:N], negU_b, sp_own, False, True, [("sp_t", r3), "cst_b"], [PS(4)])
                    mm(ps[5][:, lo:N], k_oth, qr, True, False, [("kT", b), ("qT", b)], [PS(5)])
                    mm(ps[5][:, lo:N], negU_b, sp_oth, False, False, [("sp_t", r3), "cst_b"], [PS(5)])
                    mm(ps[5][:, lo:N], nones_b, sp_own, False, True, [("sp_t", r3), "cst_b"], [PS(5)])
                    mm(ps[6][:, lo:N], ones_b, sp_own, True, False, [("sp_t", r3), "cst_b"], [PS(6)])
                    mm(ps[6][:, lo:N], ones_b, sp_oth, False, True, [("sp_t", r3), "cst_b"], [PS(6)])
                    tt("dve", tt_t[r3][:, 0, lo:N], ps[4][:, lo:N], carry[cb][:, lo:N], ALU.subtract, [PS(4), ("carry", cb)], [("tt_t", r3, 0)])
                    tt("dve", tt_t[r3][:, 1, lo:N], ps[5][:, lo:N], carry[cb][:, lo:N], ALU.subtract, [PS(5), ("carry", cb)], [("tt_t", r3, 1)])
                    tt("dve", carry[cb][:, lo:N], carry[cb][:, lo:N], ps[6][:, lo:N], ALU.add, [PS(6), ("carry", cb)], [("carry", cb)])

                def S3(n):
                    T = tasks[n]
                    lo, diag, b, k_own, k_oth, qr = geom(T)
                    r3 = n % 3
                    i = T["i"]
                    if lo > 0:
                        sc.op("pool", "memset", dict(ap=w_t[r3][:, :, 0:lo], constant=0.0), [], [("w_t", r3)])
                    act(w_t[r3][:, :, lo:N], tt_t[r3][:, :, lo:N], AF.Exp, [("tt_t", r3, 0), ("tt_t", r3, 1)], [("w_t", r3)])
                    if diag:
                        tt("pool", w_t[r3][:, 0, lo:lo + 128], w_t[r3][:, 0, lo:lo + 128], tri_b, ALU.mult,
                           [("w_t", r3), "cst_b"], [("w_t", r3)])
                    first, last = T["pi"] == 0, T["pi"] == T["np"] - 1
                    mm(ps[7][:, 0:N], Vt[b][:, i, :], w_t[r3][:, 0, 0:N], first, False, [("Vt", b), ("w_t", r3)], [PS(7)])
                    mm(ps[7][:, 0:N], Vt[b][:, NOWN + i, :], w_t[r3][:, 1, 0:N], False, last, [("Vt", b), ("w_t", r3)], [PS(7)])
                    if last:
                        h, qg = T["h"], T["qg"]
                        sc.op("dve", "tensor_copy", dict(out=ysb[:, h, qg * N:(qg + 1) * N], in_=ps[7][:, 0:N]), [PS(7)], [("ysb", h)])

                NTk = len(tasks)
                S0(0)
                for step in range(NTk + 2):
                    if step + 1 < NTk:
                        S0(step + 1)
                    if step < NTk:
                        S1(step)
                    if 1 <= step <= NTk:
                        S2(step - 1)
                    if step >= 2:
                        S3(step - 2)

                cbuf = {}
                ctasks = [(h, j) for h in range(NH) for j in range(NOWN)]
                cinfo = {}

                def C1(n):
                    h, j = ctasks[n]
                    if j == 0 and h == 0:
                        cbuf[0] = load_head(kTca_d, vca_d, qTca_d, 0)
                    if j == 2 and h + 1 < NH:
                        cbuf[h + 1] = load_head(kTca_d, vca_d, qTca_d, h + 1)
                    b = cbuf[h]
                    wl = [(0, "own", j - 2), (1, "oth", j - 1), (2, "own", j - 1), (3, "oth", j), (4, "own", j)]
                    wl = [x for x in wl if x[2] >= 0]
                    r = n % 2
                    pS0, pS1 = 0 + 2 * r, 1 + 2 * r
                    qblk = qT[b][:, j * 128:(j + 1) * 128]
                    for (wi, kind, i) in wl:
                        kslot = i if kind == "own" else NOWN + i
                        dstp = ps[pS0][:, wi * 128:(wi + 1) * 128] if wi < 4 else ps[pS1][:, 0:128]
                        dkey = PS(pS0) if wi < 4 else PS(pS1)
                        padb = (kind == "oth" and i == 0)
                        mm(dstp, kT[b][:, kslot * 128:(kslot + 1) * 128], qblk, True, False, [("kT", b), ("qT", b)], [dkey])
                        mm(dstp, ident_b, bm_b[:, (h * 5 + wi) * 128:(h * 5 + wi + 1) * 128], False, not padb,
                           ["cst_b", ("bm_b", h)], [dkey])
                        if padb:
                            mm(dstp, ident_b, pad_b[:], False, True, ["cst_b", "pad_b"], [dkey])
                    cinfo[n] = (wl, pS0, pS1, b)

                def C2(n):
                    h, j = ctasks[n]
                    wl, pS0, pS1, b = cinfo[n]
                    pr = n % 3
                    jj = j % 4
                    j0 = j - jj
                    nj = min(4, NOWN - j0)
                    if jj == 0:
                        cinfo[("pyd", h, j0)] = nxt("car", 2)
                    r2 = cinfo[("pyd", h, j0)]
                    py, pd = 4 + r2, 6 + r2
                    w0 = wl[0][0]
                    if w0 < 4:
                        act(pt[pr][:, w0 * 128:512], ps[pS0][:, w0 * 128:512], AF.Exp, [PS(pS0)], [("pt", pr)])
                    act(pt[pr][:, 512:640], ps[pS1][:, 0:128], AF.Exp, [PS(pS1)], [("pt", pr)])
                    for n_, (wi, kind, i) in enumerate(wl):
                        kslot = i if kind == "own" else NOWN + i
                        mm(ps[py][:, jj * 128:(jj + 1) * 128], Vt[b][:, kslot, :], pt[pr][:, wi * 128:(wi + 1) * 128],
                           n_ == 0, n_ == len(wl) - 1, [("Vt", b), ("pt", pr)], [PS(py)])
                    for n_, (wi, kind, i) in enumerate(wl):
                        mm(ps[pd][:, jj * 128:(jj + 1) * 128], ones_b, pt[pr][:, wi * 128:(wi + 1) * 128],
                           n_ == 0, n_ == len(wl) - 1, ["cst_b", ("pt", pr)], [PS(pd)])
                    if jj == nj - 1:
                        W = nj * 128
                        rd = nxt("rden", 2)
                        sc.op("dve", "reciprocal", dict(out=rden[rd][:, 0:W], in_=ps[pd][:, 0:W]), [PS(pd)], [("rden", rd)])
                        tt("dve", yca[:, h, j0 * 128: j0 * 128 + W], ps[py][:, 0:W], rden[rd][:, 0:W], ALU.mult,
                           [PS(py), ("rden", rd)], [("yca", h)])

                NC_ = len(ctasks)
                for step in range(NC_ + 1):
                    if step < NC_:
                        C1(step)
                    if step >= 1:
                        C2(step - 1)
                if debug:
                    for h in range(NH):
                        dma("sp", ysb_d[h], ysb[:, h, :], [("ysb", h)], ["dbg"])
                        dma("sp", yca_d[h], yca[:, h, :], [("yca", h)], ["dbg"])
                sc.barrier()
                sc.emit()

            woa = sbb("woa", [128, NH, D], BF16)
            wob = sbb("wob", [128, NH, D], BF16)
            gA = [sbb("gA%d" % i, [128, 512], BF16) for i in range(4)]
            gB = [sbb("gB%d" % i, [128, 512], BF16) for i in range(4)]
            m1 = [sbb("m1%d" % i, [128, 512], F32) for i in range(4)]
            m2 = [sbb("m2%d" % i, [128, 512], F32) for i in range(4)]
            mo = [sbb("mo%d" % i, [128, 512], BF16) for i in range(4)]
            NQW = 4 if KC % 4 == 0 else 1
            QW = D // NQW
            for q in range(NQW):
                dma("pool", woa[:, :, q * QW:(q + 1) * QW], w_o_sb[:, q * QW:(q + 1) * QW].rearrange("(h p) n -> p h n", p=128), [], [("woa", q)])
                dma("pool", wob[:, :, q * QW:(q + 1) * QW], w_o_ca[:, q * QW:(q + 1) * QW].rearrange("(h p) n -> p h n", p=128), [], [("wob", q)])
            c1steps = [(m, tg) for m in range(KC) for tg in range(SOWN // 512)]
            c1buf = {}

            def c1_loads(si):
                m, tg = c1steps[si]
                sl = slice(tg * 512, (tg + 1) * 512)
                r = nxt("c1r", 4)
                c1buf[si] = r
                dma("sp", gA[r][:], gates_d[m, :, sl], [], [("gA", r)])
                dma("sp", gB[r][:], gates_d[KC + m, :, sl], [], [("gB", r)])

            for si in range(min(3, len(c1steps))):
                c1_loads(si)
            for si, (m, tg) in enumerate(c1steps):
                if si + 3 < len(c1steps):
                    c1_loads(si + 3)
                wq = (m * 128) // QW
                sl = slice(tg * 512, (tg + 1) * 512)
                r = c1buf[si]
                pa, pb2 = 0 + r, 4 + r
                for h in range(NH):
                    mm(ps[pa][:], woa[:, h, m * 128:(m + 1) * 128], ysb[:, h, sl], h == 0, h == NH - 1, [("woa", wq)], [PS(pa)])
                for h in range(NH):
                    mm(ps[pb2][:], wob[:, h, m * 128:(m + 1) * 128], yca[:, h, sl], h == 0, h == NH - 1, [("wob", wq)], [PS(pb2)])
                tt("dve", m1[r][:], ps[pa][:], gA[r][:], ALU.mult, [PS(pa), ("gA", r)], [("m1", r)])
                tt("dve", m2[r][:], ps[pb2][:], gB[r][:], ALU.mult, [PS(pb2), ("gB", r)], [("m2", r)])
                tt("pool", mo[r][:], m1[r][:], m2[r][:], ALU.add, [("m1", r), ("m2", r)], [("mo", r)])
                dma("sp", mrg_d[m, :, sl], mo[r][:], [("mo", r)], ["dram_mrg"])
            sc.barrier()
            sc.emit()

        with contextlib.ExitStack() as phC:
            B = tile_bufs(phC, "_c")
            xT, xn, wg, xio = B["xT"], B["xn"], B["wg"], B["xio"]
            for t in range(c.NT_OWN):
                s0 = t * TT
                CH = 4 if KC % 4 == 0 else KC

                def load_mrg(s0_):
                    for k0 in range(0, KC, CH):
                        dma("sp", xn[:, k0:k0 + CH, :], mrg_d[k0:k0 + CH, :, s0_:s0_ + TT].rearrange("k p s -> p k s"), [],
                            [("xn", k, tg) for k in range(k0, k0 + CH) for tg in range(NTG)])

                if t == 0:
                    load_mrg(s0)
                for k0 in range(0, KC, CH):
                    dma("sp", xT[:, k0:k0 + CH, :], x1T_d[k0:k0 + CH, :, s0:s0 + TT].rearrange("k p s -> p k s"), [],
                        [("xT", k, tg) for k in range(k0, k0 + CH) for tg in range(NTG)])
                for gi in range(D // 256):
                    wt_, wkey = wload(B, w_out, gi * 256)
                    for cc in range(2):
                        m = gi * 2 + cc
                        for tg in range(NTG):
                            sl = slice(tg * 512, (tg + 1) * 512)
                            pb = 2 + nxt("fmps", 4)
                            for k in range(KC):
                                mm(ps[pb][:], wt_[:, k, cc * 128:(cc + 1) * 128], xn[:, k, sl], k == 0, k == KC - 1,
                                   [wkey, ("xn", k, tg)], [PS(pb)])
                            tt("dve", xT[:, m, sl], ps[pb][:], xT[:, m, sl], ALU.add, [PS(pb), ("xT", m, tg)], [("xT", m, tg)])
                rmsnorm(B, 2, True)
                ffn(B, w["ffn2_g"], w["ffn2_u"], w["ffn2_d"])
                if t + 1 < c.NT_OWN:
                    load_mrg(s0 + TT)
                rmsnorm(B, 3, False)
                for blk in range(TT // 128):
                    b = nxt("xio", 2)
                    tgk = (blk * 128) // 512
                    for q0 in range(0, KC, 4):
                        n = min(4, KC - q0)
                        pb = nxt("trps", 2)
                        for i in range(n):
                            sc.op("pe", "transpose", dict(out=ps[pb][:, i * 128:(i + 1) * 128],
                                                          in_=xT[:, q0 + i, blk * 128:(blk + 1) * 128], identity=ident_f),
                                  [("xT", q0 + i, tgk), "cst_f"], [PS(pb)])
                        if nxt("trev", 2) == 0:
                            act(xio[b][:, q0 * 128:(q0 + n) * 128], ps[pb][:, 0:n * 128], AF.Copy, [PS(pb)], [("xio", b)])
                        else:
                            sc.op("dve", "tensor_copy", dict(out=xio[b][:, q0 * 128:(q0 + n) * 128], in_=ps[pb][:, 0:n * 128]),
                                  [PS(pb)], [("xio", b)])
                    r0 = s0 + blk * 128
                    dma("sp", out_d[r0:r0 + 128, :], xio[b][:], [("xio", b)], ["dram_out"])
            sc.barrier()
            sc.emit()
    return nc


def host_consts():
    ident = np.eye(128, dtype=np.float32)
    ones = np.ones((128, 128), np.float32)
    j = np.arange(128)[:, None]
    s = np.arange(128)[None, :]
    negU = np.where(j >= s, -1.0, 0.0).astype(np.float32)
    tri = np.where(j < s, 1.0, 0.0).astype(np.float32)
    return np.concatenate([ident, ones, negU, tri, -ones], axis=1)


def ca_tables():
    wi = np.arange(5)[:, None, None]
    k = np.arange(128)[None, :, None]
    q = np.arange(128)[None, None, :]
    dist = 128 * (4 - wi) + q - k
    ridx = np.clip(dist, -63, 128) + 63
    dchunk = 2 * (4 - wi) + (q >= 64) - (k >= 64)
    valid = (dchunk >= 0) & (dchunk <= 8)
    mask = np.where(valid, 0.0, NEG).astype(np.float32)
    return ridx, mask


def make_in_maps(cfg, inputs):
    c = cfg
    D, KC, NH = c.D, c.KC, c.NH
    f = lambda a: np.ascontiguousarray(np.asarray(a, dtype=np.float32))
    x = f(inputs["x"])
    gains = np.concatenate([f(inputs[n])[0].reshape(KC, 128).T for n in ("ffn1_norm", "mix_norm", "ffn2_norm", "final_norm")], axis=1)
    bgate = f(inputs["b_gate"])[0].reshape(2 * KC, 128).T
    qkg = np.stack([f(inputs["q_norm_ca"])[0], f(inputs["k_norm_ca"])[0]], axis=1)
    ridx, mask = ca_tables()
    rb = f(inputs["rel_bias"])[0]
    biasT = rb[:, ridx]
    biasT = np.ascontiguousarray(biasT.transpose(2, 0, 1, 3)).reshape(128, NH * 5 * 128)
    maskT = np.ascontiguousarray(mask.transpose(1, 0, 2)).reshape(128, 5 * 128)
    consts = host_consts()
    shared = {
        "gains": np.ascontiguousarray(gains), "bgate": np.ascontiguousarray(bgate), "qkg": np.ascontiguousarray(qkg),
        "biasT": biasT, "maskT": maskT, "consts": consts,
        "w_in": f(inputs["w_in"])[0], "w_o_sb": f(inputs["w_o_sb"])[0], "w_o_ca": f(inputs["w_o_ca"])[0], "w_out": f(inputs["w_out"])[0],
    }
    for pre in ("ffn1", "ffn2"):
        for n in ("w_gate", "w_up", "w_down"):
            shared[pre + "_" + n] = f(inputs[pre + "_" + n])[0]
    maps = []
    for core in range(8):
        b, p = core // 2, core % 2
        xb = x[b].reshape(c.NB, 128, D)
        own = xb[p::2]
        if p == 1:
            oth = xb[0::2]
        else:
            oth = np.concatenate([np.zeros((1, 128, D), np.float32), xb[1::2][:-1]], axis=0)
        xall = np.ascontiguousarray(np.concatenate([own, oth], axis=0).reshape(c.S, D))
        pad = np.full((128, 128), NEG if p == 0 else 0.0, np.float32)
        m = dict(shared)
        m["xall"] = xall
        m["padmask"] = pad
        maps.append(m)
    return maps


def assemble(cfg, results):
    c = cfg
    out = np.zeros((4, c.S, c.D), np.float32)
    for core in range(8):
        b, p = core // 2, core % 2
        o = np.asarray(results[core]["out"], dtype=np.float32).reshape(c.NOWN, 128, c.D)
        out[b].reshape(c.NB, 128, c.D)[p::2] = o
    return out


_NC_CACHE = {}


def kernel(**inputs):
    cfg = Cfg()
    if "nc" not in _NC_CACHE:
        _NC_CACHE["nc"] = build(cfg)
    nc = _NC_CACHE["nc"]
    maps = make_in_maps(cfg, inputs)
    res = run_bass_kernel_spmd(nc, maps, core_ids=list(range(8)))
    return assemble(cfg, res.results)
```

```python
import numpy as np
import ml_dtypes
import concourse.bass as bass
import concourse.mybir as mybir
from concourse.bass_utils import run_bass_kernel_spmd

F32 = mybir.dt.float32
BF16 = mybir.dt.bfloat16
AF = mybir.ActivationFunctionType
ALU = mybir.AluOpType
NEG = -30000.0


class Cfg:
    def __init__(self, D=2048, DFF=5632, S=4096, NH=8, TT=1024, EPS=1e-6):
        self.D, self.DFF, self.S, self.NH, self.TT, self.EPS = D, DFF, S, NH, TT, EPS
        self.KC = D // 128
        self.NB = S // 128
        self.NOWN = self.NB // 2
        self.SOWN = S // 2
        self.NT = S // TT
        self.NT_OWN = self.SOWN // TT
        self.HW = NH * 128
        self.INC = 6 * self.HW + 2 * D
        self.GF = DFF // 256
        self.QB = min(4, self.NOWN)
        self.NQG = self.NOWN // self.QB
        self.NTG = TT // 512


class Sched:
    ENGS = ["pe", "act", "dve", "pool", "sp"]

    def __init__(self, nc, stack, n_dma_sems=8):
        self.nc = nc
        self.ops = []
        self.emitted = 0
        self.lastw = {}
        self.readers = {}
        self.n_dma_sems = n_dma_sems
        self.dma_rr = {e: 0 for e in self.ENGS}
        self.dma_cnt = {}
        self.dma_last = {}
        self.last_on_eng = {}
        self.cnt = {e: 0 for e in self.ENGS}
        self.seen = {e: {} for e in self.ENGS}
        self.csem = {e: stack.enter_context(nc.semaphore("c_" + e)) for e in self.ENGS}
        self.dsem = {}
        for e in ("sp", "pool"):
            for i in range(n_dma_sems):
                self.dsem[(e, i)] = stack.enter_context(nc.semaphore("d_%s%d" % (e, i)))

    def op(self, eng, meth, kw, reads=(), writes=(), dma=False):
        oid = len(self.ops)
        deps = set()
        for k in reads:
            w = self.lastw.get(k)
            if w is not None:
                deps.add(w)
        for k in writes:
            w = self.lastw.get(k)
            if w is not None:
                deps.add(w)
            for r in self.readers.get(k, ()):
                deps.add(r)
        for k in writes:
            self.lastw[k] = oid
            self.readers[k] = []
        for k in reads:
            self.readers.setdefault(k, []).append(oid)
        deps.discard(oid)
        o = dict(id=oid, eng=eng, meth=meth, kw=kw, deps=deps, dma=dma, sig=None, needed=False)
        if dma:
            i = self.dma_rr[eng]
            self.dma_rr[eng] = (i + 1) % self.n_dma_sems
            key = (eng, i)
            prev = self.dma_last.get(key)
            if prev is not None:
                deps.add(prev)
            self.dma_cnt[key] = self.dma_cnt.get(key, 0) + 1
            o["dsem"] = key
            o["dval"] = 16 * self.dma_cnt[key]
            self.dma_last[key] = oid
        self.ops.append(o)
        self.last_on_eng[eng] = oid
        return o

    def barrier(self):
        deps = set(v for v in self.last_on_eng.values() if v >= self.emitted) | set(self.dma_last.values())
        self.lastw = {}
        self.readers = {}
        b0 = self.op("sp", "nop", {}, writes=["__bar__"])
        b0["deps"] |= deps
        b0["deps"].discard(b0["id"])
        for e in self.ENGS:
            if e != "sp":
                self.op(e, "nop", {}, reads=["__bar__"])
        self.lastw = {}
        self.readers = {}

    def emit(self):
        nc = self.nc
        ops = self.ops
        new = ops[self.emitted:]
        self.emitted = len(ops)

        def pe_pe(p, o):
            return p["eng"] == "pe" and o["eng"] == "pe" and not o["dma"] and not p["dma"]

        for o in new:
            for d in o["deps"]:
                p = ops[d]
                if p["dma"] or pe_pe(p, o):
                    continue
                assert p["sig"] is not None or d >= len(ops) - len(new), "dep on already-emitted unsignalled op"
                p["needed"] = True
        for o in new:
            if (not o["dma"]) and o["needed"]:
                self.cnt[o["eng"]] += 1
                o["sig"] = self.cnt[o["eng"]]
        by_eng = {e: [o for o in new if o["eng"] == e] for e in self.ENGS}
        csem, dsem = self.csem, self.dsem

        def run(ename, eng):
            seen = self.seen[ename]
            for o in by_eng[ename]:
                need = {}
                for d in o["deps"]:
                    p = ops[d]
                    if p["dma"]:
                        s, v = dsem[p["dsem"]], p["dval"]
                    else:
                        if pe_pe(p, o):
                            continue
                        s, v = csem[p["eng"]], p["sig"]
                    if v > need.get(s, 0):
                        need[s] = v
                for s, v in need.items():
                    if v > seen.get(s, 0):
                        eng.wait_ge(s, v)
                        seen[s] = v
                ins = getattr(eng, o["meth"])(**o["kw"])
                if o["dma"]:
                    ins.then_inc(dsem[o["dsem"]], 16)
                elif o["sig"] is not None:
                    ins.then_inc(csem[ename], 1)

        with nc.Block() as block:
            block.tensor(lambda e: run("pe", e))
            block.scalar(lambda e: run("act", e))
            block.vector(lambda e: run("dve", e))
            block.gpsimd(lambda e: run("pool", e))
            block.sync(lambda e: run("sp", e))


def build(cfg, debug=False):
    import contextlib
    c = cfg
    D, DFF, S, NH, TT, KC = c.D, c.DFF, c.S, c.NH, c.TT, c.KC
    NOWN, SOWN, HW, NTG = c.NOWN, c.SOWN, c.HW, c.NTG
    nc = bass.Bass("TRN2", target_bir_lowering=False)

    def din(name, shape, dt=F32):
        return nc.dram_tensor(name, list(shape), dt, kind="ExternalInput").ap()

    scratch_kind = "ExternalOutput" if debug else "Internal"

    def dsc(name, shape, dt=BF16):
        return nc.dram_tensor(name, list(shape), dt, kind=scratch_kind).ap()

    xall = din("xall", [S, D])
    w = {}
    for pre in ("ffn1", "ffn2"):
        w[pre + "_g"] = din(pre + "_w_gate", [D, DFF])
        w[pre + "_u"] = din(pre + "_w_up", [D, DFF])
        w[pre + "_d"] = din(pre + "_w_down", [DFF, D])
    w_in = din("w_in", [D, c.INC])
    w_o_sb = din("w_o_sb", [HW, D])
    w_o_ca = din("w_o_ca", [HW, D])
    w_out = din("w_out", [D, D])
    gains_d = din("gains", [128, 4 * KC])
    bgate_d = din("bgate", [128, 2 * KC])
    qkg_d = din("qkg", [128, 2])
    biasT_d = din("biasT", [128, NH * 5 * 128])
    maskT_d = din("maskT", [128, 5 * 128])
    pad_d = din("padmask", [128, 128])
    cst_d = din("consts", [128, 5 * 128])
    out_d = nc.dram_tensor("out", [SOWN, D], F32, kind="ExternalOutput").ap()

    kTsb_d = dsc("kTsb", [NH, 128, S])
    kTca_d = dsc("kTca", [NH, 128, S])
    vsb_d = dsc("vsb", [S, HW])
    vca_d = dsc("vca", [S, HW])
    qTsb_d = dsc("qTsb", [NH, 128, SOWN])
    qTca_d = dsc("qTca", [NH, 128, SOWN])
    gates_d = dsc("gates", [2 * KC, 128, SOWN])
    x1T_d = dsc("x1T", [KC, 128, SOWN], F32)
    mrg_d = dsc("mrg", [KC, 128, SOWN])
    ysb_d = dsc("ysbd", [NH, 128, SOWN]) if debug else None
    yca_d = dsc("ycad", [NH, 128, SOWN]) if debug else None

    with contextlib.ExitStack() as top:
        sc = Sched(nc, top)

        def sb(name, shape, dt):
            return top.enter_context(nc.sbuf_tensor(name, list(shape), dt))

        pp = [top.enter_context(nc.psum_tensor("pp%d" % i, [128, 1024], F32)) for i in range(4)]
        ps = [pp[i // 2][:, (i % 2) * 512:(i % 2 + 1) * 512] for i in range(8)]
        PS = lambda i: ("ps", i)

        rot = {}

        def nxt(name, n):
            v = rot.get(name, 0)
            rot[name] = (v + 1) % n
            return v

        def mm(out, lhsT, rhs, start, stop, reads, writes):
            sc.op("pe", "matmul", dict(out=out, lhsT=lhsT, rhs=rhs, start=start, stop=stop), reads, writes)

        def act(out, in_, func, reads, writes, **kw):
            sc.op("act", "activation", dict(out=out, in_=in_, func=func, **kw), reads, writes)

        def tt(eng, out, in0, in1, op, reads, writes):
            sc.op(eng, "tensor_tensor", dict(out=out, in0=in0, in1=in1, op=op), reads, writes)

        def stt(out, in0, scalar, in1, op0, op1, reads, writes):
            sc.op("dve", "scalar_tensor_tensor", dict(out=out, in0=in0, scalar=scalar, in1=in1, op0=op0, op1=op1), reads, writes)

        def dma(q, out, in_, reads, writes):
            sc.op(q, "dma_start", dict(out=out, in_=in_), reads, writes, dma=True)

        cst_f = sb("cst_f", [128, 5 * 128], F32)
        ident_f = cst_f[:, 0:128]
        cst_b = sb("cst_b", [128, 5 * 128], BF16)
        ident_b, ones_b, negU_b, tri_b, nones_b = (cst_b[:, i * 128:(i + 1) * 128] for i in range(5))
        gains = sb("gains_sb", [128, 4 * KC], F32)
        bgate = sb("bgate_sb", [128, 2 * KC], F32)
        qkg = sb("qkg_sb", [128, 2], F32)
        qkg_s = sb("qkg_s", [128, 2], F32)
        epsD = sb("epsD", [128, 1], F32)
        one_t = sb("one_t", [128, 1], F32)
        pad_b = sb("pad_b", [128, 128], BF16)
        bm_b = sb("bm_b", [128, NH * 5 * 128], BF16)

        dma("sp", cst_f[:], cst_d, [], ["cst_f"])
        dma("sp", gains[:], gains_d, [], ["gains"])
        dma("sp", bgate[:], bgate_d, [], ["bgate"])
        dma("sp", qkg[:], qkg_d, [], ["qkg"])
        dma("pool", pad_b[:], pad_d, [], ["pad_b"])
        sc.op("dve", "tensor_copy", dict(out=cst_b[:], in_=cst_f[:]), ["cst_f"], ["cst_b"])
        sc.op("dve", "memset", dict(ap=epsD[:], constant=c.EPS), [], ["epsD"])
        sc.op("dve", "memset", dict(ap=one_t[:], constant=1.0), [], ["one_t"])
        sc.op("dve", "tensor_scalar", dict(out=qkg_s[:, 0:1], in0=qkg[:, 0:1], scalar1=128.0 ** -0.5, scalar2=None, op0=ALU.mult),
              ["qkg"], ["qkg_s0"])
        sc.op("dve", "tensor_copy", dict(out=qkg_s[:, 1:2], in_=qkg[:, 1:2]), ["qkg"], ["qkg_s1"])

        def tile_bufs(stack, sfx):
            def sa(name, shape, dt):
                return stack.enter_context(nc.sbuf_tensor(name + sfx, list(shape), dt))
            B = {}
            B["xT"] = sa("xT", [128, KC, TT], F32)
            B["xn"] = sa("xn", [128, KC, TT], BF16)
            B["wg"] = [sa("wg%d" % i, [128, KC, 256], BF16) for i in range(2)]
            B["wu"] = [sa("wu%d" % i, [128, KC, 256], BF16) for i in range(2)]
            B["wd"] = [sa("wd%d" % i, [128, 2, D], BF16) for i in range(2)]
            B["hb"] = [sa("hb%d" % i, [128, 2, TT], BF16) for i in range(2)]
            B["xio"] = [sa("xio%d" % i, [128, D], F32) for i in range(2)]
            B["sq"] = [sa("sq%d" % i, [128, 512], BF16) for i in range(2)]
            B["rs"] = sa("rs", [128, 512], F32)
            B["rsn"] = [sa("rsn%d" % i, [128, 512], F32) for i in range(2)]
            B["sil"] = [sa("sil%d" % i, [128, 512], F32) for i in range(2)]
            B["ost"] = [sa("ost%d" % i, [128, 512], BF16) for i in range(2)]
            B["vst"] = sa("vst", [128, TT // 128, 256], BF16)
            return B

        XT_ALL = [("xT", m, tg) for m in range(KC) for tg in range(NTG)]
        XN_ALL = [("xn", k, tg) for k in range(KC) for tg in range(NTG)]

        xbuf = {}

        def issue_x(B, t, blk):
            b = nxt("xio", 2)
            xbuf[(t, blk)] = b
            r0 = t * TT + blk * 128
            dma("sp", B["xio"][b][:], xall[r0:r0 + 128, :], [], [("xio", b)])

        def transpose_blk(B, t, blk):
            xT, xio = B["xT"], B["xio"]
            b = xbuf[(t, blk)]
            tgk = (blk * 128) // 512
            for q0 in range(0, KC, 4):
                n = min(4, KC - q0)
                pb = 6 + nxt("trps", 2)
                for i in range(n):
                    sc.op("pe", "transpose", dict(out=ps[pb][:, i * 128:(i + 1) * 128],
                                                  in_=xio[b][:, (q0 + i) * 128:(q0 + i + 1) * 128], identity=ident_f),
                          [("xio", b), "cst_f"], [PS(pb)])
                src = ps[pb][:, 0:n * 128].rearrange("p (a b) -> p a b", a=n)
                dst = xT[:, q0:q0 + n, blk * 128:(blk + 1) * 128]
                wr = [("xT", m, tgk) for m in range(q0, q0 + n)]
                if nxt("trev", 2) == 0:
                    act(dst, src, AF.Copy, [PS(pb)], wr)
                else:
                    sc.op("dve", "tensor_copy", dict(out=dst, in_=src), [PS(pb)], wr)

        def load_steps(B, t):
            nb_ = TT // 128

            def mk(k):
                def f():
                    transpose_blk(B, t, k)
                    if k + 2 < nb_:
                        issue_x(B, t, k + 2)
                return f
            return [mk(k) for k in range(nb_)]

        def load_prologue(B, t):
            issue_x(B, t, 0)
            if TT // 128 > 1:
                issue_x(B, t, 1)

        def load_transpose(B, t):
            load_prologue(B, t)
            for f in load_steps(B, t):
                f()

        def rmsnorm(B, which, to_xn):
            xT, xn, sq, rs = B["xT"], B["xn"], B["sq"], B["rs"]
            for tg in range(NTG):
                sl = slice(tg * 512, (tg + 1) * 512)
                for k in range(KC):
                    s_ = nxt("sq", 2)
                    act(sq[s_][:], xT[:, k, sl], AF.Square, [("xT", k, tg)], [("sq", s_)])
                    mm(ps[1][:], ones_b, sq[s_][:], k == 0, k == KC - 1, [("sq", s_), "cst_b"], [PS(1)])
                act(rs[:], ps[1][:], AF.Sqrt, [PS(1), "epsD"], ["rs"], scale=1.0 / D, bias=epsD[:, 0:1])
                sc.op("dve", "reciprocal", dict(out=rs[:], in_=rs[:]), ["rs"], ["rs"])
                for k in range(KC):
                    gcol = gains[:, which * KC + k: which * KC + k + 1]
                    if to_xn:
                        stt(xn[:, k, sl], xT[:, k, sl], gcol, rs[:], ALU.mult, ALU.mult, [("xT", k, tg), "rs", "gains"], [("xn", k, tg)])
                    else:
                        stt(xT[:, k, sl], xT[:, k, sl], gcol, rs[:], ALU.mult, ALU.mult, [("xT", k, tg), "rs", "gains"], [("xT", k, tg)])

        wcnt = {"g": 0}

        def ffn(B, Wg, Wu, Wd):
            xT, xn, wg, wu, wd, hb, sil = B["xT"], B["xn"], B["wg"], B["wu"], B["wd"], B["hb"], B["sil"]

            def down(g, slot, pairs):
                hp = g % 2
                for (m, tg) in pairs:
                    sl = slice(tg * 512, (tg + 1) * 512)
                    pb = 6 + nxt("dps", 2)
                    for cc in range(2):
                        mm(ps[pb][:], wd[slot][:, cc, m * 128:(m + 1) * 128], hb[hp][:, cc, sl], cc == 0, cc == 1,
                           [("wd", slot), ("hb", hp, cc, tg)], [PS(pb)])
                    stt(xT[:, m, sl], ps[pb][:], 0.5, xT[:, m, sl], ALU.mult, ALU.add, [PS(pb), ("xT", m, tg)], [("xT", m, tg)])

            all_pairs = [(m, tg) for m in range(KC) for tg in range(NTG)]
            combos = [(cc, tg) for cc in range(2) for tg in range(NTG)]
            npart = len(combos)
            parts = [all_pairs[i * len(all_pairs) // npart:(i + 1) * len(all_pairs) // npart] for i in range(npart)]
            prev = None
            for g in range(c.GF):
                slot = wcnt["g"] % 2
                wcnt["g"] += 1
                c0 = g * 256
                dma("pool", wg[slot][:], Wg[:, c0:c0 + 256].rearrange("(k p) n -> p k n", p=128), [], [("wg", slot)])
                dma("pool", wu[slot][:], Wu[:, c0:c0 + 256].rearrange("(k p) n -> p k n", p=128), [], [("wu", slot)])
                dma("pool", wd[slot][:], Wd[c0:c0 + 256, :].rearrange("(k p) n -> p k n", p=128), [], [("wd", slot)])
                for ci, (cc, tg) in enumerate(combos):
                    sl = slice(tg * 512, (tg + 1) * 512)
                    r = nxt("gups", 2)
                    pg, pu = 2 + r, 4 + r
                    dl = list(parts[ci]) if prev is not None else []
                    nmm = 2 * KC
                    every = max(1, nmm // max(1, len(dl))) if dl else 0
                    cnt = 0
                    for (pbk, wsrc, wkey) in ((pg, wg, "wg"), (pu, wu, "wu")):
                        for k in range(KC):
                            mm(ps[pbk][:], wsrc[slot][:, k, cc * 128:(cc + 1) * 128], xn[:, k, sl], k == 0, k == KC - 1,
                               [(wkey, slot), ("xn", k, tg)], [PS(pbk)])
                            cnt += 1
                            if dl and cnt % every == 0:
                                down(prev[0], prev[1], [dl.pop(0)])
                    act(sil[r][:], ps[pg][:], AF.Silu, [PS(pg)], [("sil", r)])
                    tt("dve", hb[g % 2][:, cc, sl], sil[r][:], ps[pu][:], ALU.mult, [("sil", r), PS(pu)], [("hb", g % 2, cc, tg)])
                    if dl:
                        down(prev[0], prev[1], dl)
                prev = (g, slot)
            down(prev[0], prev[1], [(m, tg) for tg in range(NTG) for m in range(KC)])

        def wload(B, Wap, col0):
            r = wcnt["r"] = (wcnt.get("r", -1) + 1) % 4
            name, slot = ("wg", r) if r < 2 else ("wu", r - 2)
            tile_ = B[name][slot]
            dma("pool", tile_[:], Wap[:, col0:col0 + 256].rearrange("(k p) n -> p k n", p=128), [], [(name, slot)])
            return tile_, (name, slot)

        pend = []

        def flush_pend():
            while pend:
                f, a = pend.pop(0)
                f(*a)

        def fm_group(B, Wap, col0, evac):
            xn = B["xn"]
            wt_, wkey = wload(B, Wap, col0)
            for cc in range(2):
                for tg in range(NTG):
                    sl = slice(tg * 512, (tg + 1) * 512)
                    pb = 2 + nxt("fmps", 4)
                    for k in range(KC):
                        mm(ps[pb][:], wt_[:, k, cc * 128:(cc + 1) * 128], xn[:, k, sl], k == 0, k == KC - 1,
                           [wkey, ("xn", k, tg)], [PS(pb)])
                    flush_pend()
                    pend.append((evac, (cc, tg, pb)))

        def win_phase(B, t, filler=None):
            xn, wg, sq, rs, ost, vst = B["xn"], B["wg"], B["sq"], B["rs"], B["ost"], B["vst"]
            own = t < c.NT_OWN
            s0 = t * TT

            def plain(dst, scale):
                def ev(gidx):
                    def f(cc, tg, pb):
                        head = gidx * 2 + cc
                        ob = nxt("ost", 2)
                        act(ost[ob][:], ps[pb][:], AF.Copy, [PS(pb)], [("ost", ob)], scale=scale)
                        dma("sp", dst[head, :, s0 + tg * 512: s0 + (tg + 1) * 512], ost[ob][:], [("ost", ob)], ["dram_qk"])
                    return f
                return ev

            def normed(dst, gi_):
                def ev(gidx):
                    def f(cc, tg, pb):
                        head = gidx * 2 + cc
                        s_ = nxt("sq", 2)
                        rb = nxt("rsn", 2)
                        rsb = B["rsn"][rb]
                        act(sq[s_][:], ps[pb][:], AF.Square, [PS(pb)], [("sq", s_)])
                        mm(ps[rb][:], ones_b, sq[s_][:], True, True, [("sq", s_), "cst_b"], [PS(rb)])
                        act(rsb[:], ps[rb][:], AF.Sqrt, [PS(rb), "epsD"], [("rsn", rb)], scale=1.0 / 128, bias=epsD[:, 0:1])
                        sc.op("dve", "reciprocal", dict(out=rsb[:], in_=rsb[:]), [("rsn", rb)], [("rsn", rb)])
                        ob = nxt("ost", 2)
                        stt(ost[ob][:], ps[pb][:], qkg_s[:, gi_:gi_ + 1], rsb[:], ALU.mult, ALU.mult,
                            [PS(pb), ("rsn", rb), "qkg_s0", "qkg_s1"], [("ost", ob)])
                        dma("sp", dst[head, :, s0 + tg * 512: s0 + (tg + 1) * 512], ost[ob][:], [("ost", ob)], ["dram_qk"])
                    return f
                return ev

            def gate_ev(gidx):
                def f(cc, tg, pb):
                    ch = gidx * 2 + cc
                    ob = nxt("ost", 2)
                    act(ost[ob][:], ps[pb][:], AF.Sigmoid, [PS(pb), "bgate"], [("ost", ob)], bias=bgate[:, ch:ch + 1])
                    dma("sp", gates_d[ch, :, s0 + tg * 512: s0 + (tg + 1) * 512], ost[ob][:], [("ost", ob)], ["dram_gates"])
                return f

            sections = []
            if own:
                sections.append((0, HW // 256, plain(qTsb_d, 128.0 ** -0.5)))
            sections.append((HW, HW // 256, plain(kTsb_d, 1.0)))
            if own:
                sections.append((3 * HW, HW // 256, normed(qTca_d, 0)))
            sections.append((4 * HW, HW // 256, normed(kTca_d, 1)))
            if own:
                sections.append((6 * HW, 2 * D // 256, gate_ev))
            ngroups = sum(x[1] for x in sections) + 2 * (HW // 256)
            fill_every = max(1, ngroups // (TT // 128 + 1))
            gcount = [0]

            def fill():
                gcount[0] += 1
                if filler and gcount[0] % fill_every == 0:
                    filler.pop(0)()

            for (cbase, ng, evf) in sections:
                for gi in range(ng):
                    fm_group(B, w_in, cbase + gi * 256, evf(gi))
                    fill()
            flush_pend()
            for (cbase, vd) in ((2 * HW, vsb_d), (5 * HW, vca_d)):
                for gi in range(HW // 256):
                    wt_, wkey = wload(B, w_in, cbase + gi * 256)
                    for tb in range(TT // 128):
                        pb = 2 + nxt("fmps", 4)
                        tgk = (tb * 128) // 512
                        for k in range(KC):
                            mm(ps[pb][:, 0:256], xn[:, k, tb * 128:(tb + 1) * 128], wt_[:, k, :], k == 0, k == KC - 1,
                               [wkey, ("xn", k, tgk)], [PS(pb)])
                        if nxt("vev", 2) == 0:
                            act(vst[:, tb, :], ps[pb][:, 0:256], AF.Copy, [PS(pb)], [("vst", tb)])
                        else:
                            sc.op("dve", "tensor_copy", dict(out=vst[:, tb, :], in_=ps[pb][:, 0:256]), [PS(pb)], [("vst", tb)])
                    dst = vd[s0:s0 + TT, gi * 256:(gi + 1) * 256].rearrange("(b p) n -> p b n", p=128)
                    dma("sp", dst, vst[:], [("vst", tb) for tb in range(TT // 128)], ["dram_v"])
                    fill()
            while filler:
                filler.pop(0)()

        with contextlib.ExitStack() as phA:
            B = tile_bufs(phA, "_a")
            load_transpose(B, 0)
            for t in range(c.NT):
                rmsnorm(B, 0, True)
                ffn(B, w["ffn1_g"], w["ffn1_u"], w["ffn1_d"])
                if t < c.NT_OWN:
                    dma("sp", x1T_d[:, :, t * TT:(t + 1) * TT].rearrange("k p s -> p k s"), B["xT"][:], XT_ALL, ["dram_x1"])
                rmsnorm(B, 1, True)
                filler = []
                if t + 1 < c.NT:
                    load_prologue(B, t + 1)
                    filler = load_steps(B, t + 1)
                win_phase(B, t, filler)
            sc.barrier()
            sc.emit()

        with contextlib.ExitStack() as phB:
            def sbb(name, shape, dt):
                return phB.enter_context(nc.sbuf_tensor(name, list(shape), dt))

            ysb = sbb("ysb", [128, NH, SOWN], BF16)
            yca = sbb("yca", [128, NH, SOWN], BF16)
            with contextlib.ExitStack() as phB1:
                def sb1(name, shape, dt):
                    return phB1.enter_context(nc.sbuf_tensor(name, list(shape), dt))

                kT = [sb1("kT%d" % i, [128, S], BF16) for i in range(2)]
                Vt = [sb1("Vt%d" % i, [128, c.NB, 128], BF16) for i in range(2)]
                qT = [sb1("qT%d" % i, [128, SOWN], BF16) for i in range(2)]
                e_t = [sb1("e_t%d" % i, [128, 2, 512], F32) for i in range(2)]
                sp_t = [sb1("sp_t%d" % i, [128, 2, 512], BF16) for i in range(3)]
                tt_t = [sb1("tt_t%d" % i, [128, 2, 512], F32) for i in range(3)]
                w_t = [sb1("w_t%d" % i, [128, 2, 512], BF16) for i in range(3)]
                carry = [sb1("carry%d" % i, [128, 512], F32) for i in range(2)]
                pt = [sb1("pt%d" % i, [128, 640], BF16) for i in range(3)]
                rden = [sb1("rden%d" % i, [128, 512], F32) for i in range(2)]
                bm_f = [sb1("bm_f%d" % i, [128, 640], F32) for i in range(2)]
                mk_f = sb1("mk_f", [128, 5 * 128], F32)

                dma("sp", mk_f[:], maskT_d, [], ["mk_f"])
                for h in range(NH):
                    bb = nxt("bmf", 2)
                    dma("sp", bm_f[bb][:], biasT_d[:, h * 640:(h + 1) * 640], [], [("bm_f", bb)])
                    tt("dve", bm_b[:, h * 640:(h + 1) * 640], bm_f[bb][:], mk_f[:], ALU.add, [("bm_f", bb), "mk_f"], [("bm_b", h)])

                def load_head(kd, vd, qd, h):
                    b = nxt("hbuf", 2)
                    dma("sp", kT[b][:], kd[h], [], [("kT", b)])
                    dma("sp", Vt[b][:], vd[:, h * 128:(h + 1) * 128].rearrange("(b p) n -> p b n", p=128), [], [("Vt", b)])
                    dma("sp", qT[b][:], qd[h], [], [("qT", b)])
                    return b

                N = c.QB * 128
                hbuf = {}
                tasks = []
                for h in range(NH):
                    for qg in range(c.NQG):
                        npair = c.QB * (qg + 1)
                        for pi, i in enumerate(range(npair - 1, -1, -1)):
                            tasks.append(dict(h=h, qg=qg, pi=pi, np=npair, i=i, first_of_head=(qg == 0 and pi == 0)))
                qgc = [0]
                pp3 = [pp[i][:].rearrange("p (a n) -> p a n", a=2) for i in range(4)]

                def geom(T):
                    i, qg = T["i"], T["qg"]
                    lo = max(0, i - c.QB * qg) * 128
                    diag = i >= c.QB * qg
                    b = hbuf[T["h"]]
                    k_own = kT[b][:, i * 128:(i + 1) * 128]
                    k_oth = kT[b][:, (NOWN + i) * 128:(NOWN + i + 1) * 128]
                    qr = qT[b][:, qg * N + lo: qg * N + N]
                    return lo, diag, b, k_own, k_oth, qr

                def S0(n):
                    T = tasks[n]
                    if T["first_of_head"] and T["h"] == 0:
                        hbuf[0] = load_head(kTsb_d, vsb_d, qTsb_d, 0)
                    if T["qg"] == 0 and T["pi"] == 3 and T["h"] + 1 < NH:
                        hbuf[T["h"] + 1] = load_head(kTsb_d, vsb_d, qTsb_d, T["h"] + 1)
                    lo, diag, b, k_own, k_oth, qr = geom(T)
                    a_ = n % 2
                    mm(ps[2 * a_][:, lo:N], k_own, qr, True, True, [("kT", b), ("qT", b)], [PS(2 * a_)])
                    mm(ps[2 * a_ + 1][:, lo:N], k_oth, qr, True, True, [("kT", b), ("qT", b)], [PS(2 * a_ + 1)])

                def S1(n):
                    T = tasks[n]
                    lo, diag, b, k_own, k_oth, qr = geom(T)
                    a_, r3 = n % 2, n % 3
                    act(e_t[a_][:, :, lo:N], pp3[a_][:, :, lo:N], AF.Exp, [PS(2 * a_), PS(2 * a_ + 1)], [("e_t", a_)])
                    act(sp_t[r3][:, :, lo:N], e_t[a_][:, :, lo:N], AF.Ln, [("e_t", a_)], [("sp_t", r3)], bias=1.0)
                    if diag:
                        tt("pool", sp_t[r3][:, 0, lo:lo + 128], sp_t[r3][:, 0, lo:lo + 128], tri_b, ALU.mult,
                           [("sp_t", r3), "cst_b"], [("sp_t", r3)])

                def S2(n):
                    T = tasks[n]
                    lo, diag, b, k_own, k_oth, qr = geom(T)
                    r3 = n % 3
                    if T["pi"] == 0:
                        qgc[0] += 1
                        T["cb"] = qgc[0] % 2
                        sc.op("pool", "memset", dict(ap=carry[T["cb"]][:, 0:N], constant=0.0), [], [("carry", T["cb"])])
                    else:
                        T["cb"] = tasks[n - 1]["cb"]
                    cb = T["cb"]
                    sp_own, sp_oth = sp_t[r3][:, 0, lo:N], sp_t[r3][:, 1, lo:N]
                    mm(ps[4][:, lo:N], k_own, qr, True, False, [("kT", b), ("qT", b)], [PS(4)])
                    mm(ps[4][:, lo:N], negU_b, sp_own, False, True, [("sp_t", r3), "cst_b"], [PS(4)])
                    mm(ps[5][:, lo:N], k_oth, qr, True, False, [("kT", b), ("qT", b)], [PS(5)])
                    mm(ps[5][:, lo:N], negU_b, sp_oth, False, False, [("sp_t", r3), "cst_b"], [PS(5)])
                    mm(ps[5][:, lo:N], nones_b, sp_own, False, True, [("sp_t", r3), "cst_b"], [PS(5)])
                    mm(ps[6][:, lo:N], ones_b, sp_own, True, False, [("sp_t", r3), "cst_b"], [PS(6)])
                    mm(ps[6][:, lo:N], ones_b, sp_oth, False, True, [("sp_t", r3), "cst_b"], [PS(6)])
                    tt("dve", tt_t[r3][:, 0, lo:N], ps[4][:, lo:N], carry[cb][:, lo:N], ALU.subtract, [PS(4), ("carry", cb)], [("tt_t", r3, 0)])
                    tt("dve", tt_t[r3][:, 1, lo:N], ps[5][:, lo:N], carry[cb][:, lo:N], ALU.subtract, [PS(5), ("carry", cb)], [("tt_t", r3, 1)])
                    tt("dve", carry[cb][:, lo:N], carry[cb][:, lo:N], ps[6][:, lo:N], ALU.add, [PS(6), ("carry", cb)], [("carry", cb)])

                def S3(n):
                    T = tasks[n]
                    lo, diag, b, k_own, k_oth, qr = geom(T)
                    r3 = n % 3
                    i = T["i"]
                    if lo > 0:
                        sc.op("pool", "memset", dict(ap=w_t[r3][:, :, 0:lo], constant=0.0), [], [("w_t", r3)])
                    act(w_t[r3][:, :, lo:N], tt_t[r3][:, :, lo:N], AF.Exp, [("tt_t", r3, 0), ("tt_t", r3, 1)], [("w_t", r3)])
                    if diag:
                        tt("pool", w_t[r3][:, 0, lo:lo + 128], w_t[r3][:, 0, lo:lo + 128], tri_b, ALU.mult,
                           [("w_t", r3), "cst_b"], [("w_t", r3)])
                    first, last = T["pi"] == 0, T["pi"] == T["np"] - 1
                    mm(ps[7][:, 0:N], Vt[b][:, i, :], w_t[r3][:, 0, 0:N], first, False, [("Vt", b), ("w_t", r3)], [PS(7)])
                    mm(ps[7][:, 0:N], Vt[b][:, NOWN + i, :], w_t[r3][:, 1, 0:N], False, last, [("Vt", b), ("w_t", r3)], [PS(7)])
                    if last:
                        h, qg = T["h"], T["qg"]
                        sc.op("dve", "tensor_copy", dict(out=ysb[:, h, qg * N:(qg + 1) * N], in_=ps[7][:, 0:N]), [PS(7)], [("ysb", h)])

                NTk = len(tasks)
                S0(0)
                for step in range(NTk + 2):
                    if step + 1 < NTk:
                        S0(step + 1)
                    if step < NTk:
                        S1(step)
                    if 1 <= step <= NTk:
                        S2(step - 1)
                    if step >= 2:
                        S3(step - 2)

                cbuf = {}
                ctasks = [(h, j) for h in range(NH) for j in range(NOWN)]
                cinfo = {}

                def C1(n):
                    h, j = ctasks[n]
                    if j == 0 and h == 0:
                        cbuf[0] = load_head(kTca_d, vca_d, qTca_d, 0)
                    if j == 2 and h + 1 < NH:
                        cbuf[h + 1] = load_head(kTca_d, vca_d, qTca_d, h + 1)
                    b = cbuf[h]
                    wl = [(0, "own", j - 2), (1, "oth", j - 1), (2, "own", j - 1), (3, "oth", j), (4, "own", j)]
                    wl = [x for x in wl if x[2] >= 0]
                    r = n % 2
                    pS0, pS1 = 0 + 2 * r, 1 + 2 * r
                    qblk = qT[b][:, j * 128:(j + 1) * 128]
                    for (wi, kind, i) in wl:
                        kslot = i if kind == "own" else NOWN + i
                        dstp = ps[pS0][:, wi * 128:(wi + 1) * 128] if wi < 4 else ps[pS1][:, 0:128]
                        dkey = PS(pS0) if wi < 4 else PS(pS1)
                        padb = (kind == "oth" and i == 0)
                        mm(dstp, kT[b][:, kslot * 128:(kslot + 1) * 128], qblk, True, False, [("kT", b), ("qT", b)], [dkey])
                        mm(dstp, ident_b, bm_b[:, (h * 5 + wi) * 128:(h * 5 + wi + 1) * 128], False, not padb,
                           ["cst_b", ("bm_b", h)], [dkey])
                        if padb:
                            mm(dstp, ident_b, pad_b[:], False, True, ["cst_b", "pad_b"], [dkey])
                    cinfo[n] = (wl, pS0, pS1, b)

                def C2(n):
                    h, j = ctasks[n]
                    wl, pS0, pS1, b = cinfo[n]
                    pr = n % 3
                    jj = j % 4
                    j0 = j - jj
                    nj = min(4, NOWN - j0)
                    if jj == 0:
                        cinfo[("pyd", h, j0)] = nxt("car", 2)
                    r2 = cinfo[("pyd", h, j0)]
                    py, pd = 4 + r2, 6 + r2
                    w0 = wl[0][0]
                    if w0 < 4:
                        act(pt[pr][:, w0 * 128:512], ps[pS0][:, w0 * 128:512], AF.Exp, [PS(pS0)], [("pt", pr)])
                    act(pt[pr][:, 512:640], ps[pS1][:, 0:128], AF.Exp, [PS(pS1)], [("pt", pr)])
                    for n_, (wi, kind, i) in enumerate(wl):
                        kslot = i if kind == "own" else NOWN + i
                        mm(ps[py][:, jj * 128:(jj + 1) * 128], Vt[b][:, kslot, :], pt[pr][:, wi * 128:(wi + 1) * 128],
                           n_ == 0, n_ == len(wl) - 1, [("Vt", b), ("pt", pr)], [PS(py)])
                    for n_, (wi, kind, i) in enumerate(wl):
                        mm(ps[pd][:, jj * 128:(jj + 1) * 128], ones_b, pt[pr][:, wi * 128:(wi + 1) * 128],
                           n_ == 0, n_ == len(wl) - 1, ["cst_b", ("pt", pr)], [PS(pd)])
                    if jj == nj - 1:
                        W = nj * 128
                        rd = nxt("rden", 2)
                        sc.op("dve", "reciprocal", dict(out=rden[rd][:, 0:W], in_=ps[pd][:, 0:W]), [PS(pd)], [("rden", rd)])
                        tt("dve", yca[:, h, j0 * 128: j0 * 128 + W], ps[py][:, 0:W], rden[rd][:, 0:W], ALU.mult,
                           [PS(py), ("rden", rd)], [("yca", h)])

                NC_ = len(ctasks)
                for step in range(NC_ + 1):
                    if step < NC_:
                        C1(step)
                    if step >= 1:
                        C2(step - 1)
                if debug:
                    for h in range(NH):
                        dma("sp", ysb_d[h], ysb[:, h, :], [("ysb", h)], ["dbg"])
                        dma("sp", yca_d[h], yca[:, h, :], [("yca", h)], ["dbg"])
                sc.barrier()
                sc.emit()

            woa = sbb("woa", [128, NH, D], BF16)
            wob = sbb("wob", [128, NH, D], BF16)
            gA = [sbb("gA%d" % i, [128, 512], BF16) for i in range(4)]
            gB = [sbb("gB%d" % i, [128, 512], BF16) for i in range(4)]
            m1 = [sbb("m1%d" % i, [128, 512], F32) for i in range(4)]
            m2 = [sbb("m2%d" % i, [128, 512], F32) for i in range(4)]
            mo = [sbb("mo%d" % i, [128, 512], BF16) for i in range(4)]
            NQW = 4 if KC % 4 == 0 else 1
            QW = D // NQW
            for q in range(NQW):
                dma("pool", woa[:, :, q * QW:(q + 1) * QW], w_o_sb[:, q * QW:(q + 1) * QW].rearrange("(h p) n -> p h n", p=128), [], [("woa", q)])
                dma("pool", wob[:, :, q * QW:(q + 1) * QW], w_o_ca[:, q * QW:(q + 1) * QW].rearrange("(h p) n -> p h n", p=128), [], [("wob", q)])
            c1steps = [(m, tg) for m in range(KC) for tg in range(SOWN // 512)]
            c1buf = {}

            def c1_loads(si):
                m, tg = c1steps[si]
                sl = slice(tg * 512, (tg + 1) * 512)
                r = nxt("c1r", 4)
                c1buf[si] = r
                dma("sp", gA[r][:], gates_d[m, :, sl], [], [("gA", r)])
                dma("sp", gB[r][:], gates_d[KC + m, :, sl], [], [("gB", r)])

            for si in range(min(3, len(c1steps))):
                c1_loads(si)
            for si, (m, tg) in enumerate(c1steps):
                if si + 3 < len(c1steps):
                    c1_loads(si + 3)
                wq = (m * 128) // QW
                sl = slice(tg * 512, (tg + 1) * 512)
                r = c1buf[si]
                pa, pb2 = 0 + r, 4 + r
                for h in range(NH):
                    mm(ps[pa][:], woa[:, h, m * 128:(m + 1) * 128], ysb[:, h, sl], h == 0, h == NH - 1, [("woa", wq)], [PS(pa)])
                for h in range(NH):
                    mm(ps[pb2][:], wob[:, h, m * 128:(m + 1) * 128], yca[:, h, sl], h == 0, h == NH - 1, [("wob", wq)], [PS(pb2)])
                tt("dve", m1[r][:], ps[pa][:], gA[r][:], ALU.mult, [PS(pa), ("gA", r)], [("m1", r)])
                tt("dve", m2[r][:], ps[pb2][:], gB[r][:], ALU.mult, [PS(pb2), ("gB", r)], [("m2", r)])
                tt("pool", mo[r][:], m1[r][:], m2[r][:], ALU.add, [("m1", r), ("m2", r)], [("mo", r)])
                dma("sp", mrg_d[m, :, sl], mo[r][:], [("mo", r)], ["dram_mrg"])
            sc.barrier()
            sc.emit()

        with contextlib.ExitStack() as phC:
            B = tile_bufs(phC, "_c")
            xT, xn, wg, xio = B["xT"], B["xn"], B["wg"], B["xio"]
            for t in range(c.NT_OWN):
                s0 = t * TT
                CH = 4 if KC % 4 == 0 else KC
                for k0 in range(0, KC, CH):
                    dma("sp", xn[:, k0:k0 + CH, :], mrg_d[k0:k0 + CH, :, s0:s0 + TT].rearrange("k p s -> p k s"), [],
                        [("xn", k, tg) for k in range(k0, k0 + CH) for tg in range(NTG)])
                for k0 in range(0, KC, CH):
                    dma("sp", xT[:, k0:k0 + CH, :], x1T_d[k0:k0 + CH, :, s0:s0 + TT].rearrange("k p s -> p k s"), [],
                        [("xT", k, tg) for k in range(k0, k0 + CH) for tg in range(NTG)])
                for gi in range(D // 256):
                    wt_, wkey = wload(B, w_out, gi * 256)
                    for cc in range(2):
                        m = gi * 2 + cc
                        for tg in range(NTG):
                            sl = slice(tg * 512, (tg + 1) * 512)
                            pb = 2 + nxt("fmps", 4)
                            for k in range(KC):
                                mm(ps[pb][:], wt_[:, k, cc * 128:(cc + 1) * 128], xn[:, k, sl], k == 0, k == KC - 1,
                                   [wkey, ("xn", k, tg)], [PS(pb)])
                            tt("dve", xT[:, m, sl], ps[pb][:], xT[:, m, sl], ALU.add, [PS(pb), ("xT", m, tg)], [("xT", m, tg)])
                rmsnorm(B, 2, True)
                ffn(B, w["ffn2_g"], w["ffn2_u"], w["ffn2_d"])
                rmsnorm(B, 3, False)
                for blk in range(TT // 128):
                    b = nxt("xio", 2)
                    tgk = (blk * 128) // 512
                    for q0 in range(0, KC, 4):
                        n = min(4, KC - q0)
                        pb = nxt("trps", 2)
                        for i in range(n):
                            sc.op("pe", "transpose", dict(out=ps[pb][:, i * 128:(i + 1) * 128],
                                                          in_=xT[:, q0 + i, blk * 128:(blk + 1) * 128], identity=ident_f),
                                  [("xT", q0 + i, tgk), "cst_f"], [PS(pb)])
                        if nxt("trev", 2) == 0:
                            act(xio[b][:, q0 * 128:(q0 + n) * 128], ps[pb][:, 0:n * 128], AF.Copy, [PS(pb)], [("xio", b)])
                        else:
                            sc.op("dve", "tensor_copy", dict(out=xio[b][:, q0 * 128:(q0 + n) * 128], in_=ps[pb][:, 0:n * 128]),
                                  [PS(pb)], [("xio", b)])
                    r0 = s0 + blk * 128
                    dma("sp", out_d[r0:r0 + 128, :], xio[b][:], [("xio", b)], ["dram_out"])
            sc.barrier()
            sc.emit()
    return nc


def host_consts():
    ident = np.eye(128, dtype=np.float32)
    ones = np.ones((128, 128), np.float32)
    j = np.arange(128)[:, None]
    s = np.arange(128)[None, :]
    negU = np.where(j >= s, -1.0, 0.0).astype(np.float32)
    tri = np.where(j < s, 1.0, 0.0).astype(np.float32)
    return np.concatenate([ident, ones, negU, tri, -ones], axis=1)


def ca_tables():
    wi = np.arange(5)[:, None, None]
    k = np.arange(128)[None, :, None]
    q = np.arange(128)[None, None, :]
    dist = 128 * (4 - wi) + q - k
    ridx = np.clip(dist, -63, 128) + 63
    dchunk = 2 * (4 - wi) + (q >= 64) - (k >= 64)
    valid = (dchunk >= 0) & (dchunk <= 8)
    mask = np.where(valid, 0.0, NEG).astype(np.float32)
    return ridx, mask


def make_in_maps(cfg, inputs):
    c = cfg
    D, KC, NH = c.D, c.KC, c.NH
    f = lambda a: np.ascontiguousarray(np.asarray(a, dtype=np.float32))
    x = f(inputs["x"])
    gains = np.concatenate([f(inputs[n])[0].reshape(KC, 128).T for n in ("ffn1_norm", "mix_norm", "ffn2_norm", "final_norm")], axis=1)
    bgate = f(inputs["b_gate"])[0].reshape(2 * KC, 128).T
    qkg = np.stack([f(inputs["q_norm_ca"])[0], f(inputs["k_norm_ca"])[0]], axis=1)
    ridx, mask = ca_tables()
    rb = f(inputs["rel_bias"])[0]
    biasT = rb[:, ridx]
    biasT = np.ascontiguousarray(biasT.transpose(2, 0, 1, 3)).reshape(128, NH * 5 * 128)
    maskT = np.ascontiguousarray(mask.transpose(1, 0, 2)).reshape(128, 5 * 128)
    consts = host_consts()
    shared = {
        "gains": np.ascontiguousarray(gains), "bgate": np.ascontiguousarray(bgate), "qkg": np.ascontiguousarray(qkg),
        "biasT": biasT, "maskT": maskT, "consts": consts,
        "w_in": f(inputs["w_in"])[0], "w_o_sb": f(inputs["w_o_sb"])[0], "w_o_ca": f(inputs["w_o_ca"])[0], "w_out": f(inputs["w_out"])[0],
    }
    for pre in ("ffn1", "ffn2"):
        for n in ("w_gate", "w_up", "w_down"):
            shared[pre + "_" + n] = f(inputs[pre + "_" + n])[0]
    maps = []
    for core in range(8):
        b, p = core // 2, core % 2
        xb = x[b].reshape(c.NB, 128, D)
        own = xb[p::2]
        if p == 1:
            oth = xb[0::2]
        else:
            oth = np.concatenate([np.zeros((1, 128, D), np.float32), xb[1::2][:-1]], axis=0)
        xall = np.ascontiguousarray(np.concatenate([own, oth], axis=0).reshape(c.S, D))
        pad = np.full((128, 128), NEG if p == 0 else 0.0, np.float32)
        m = dict(shared)
        m["xall"] = xall
        m["padmask"] = pad
        maps.append(m)
    return maps


def assemble(cfg, results):
    c = cfg
    out = np.zeros((4, c.S, c.D), np.float32)
    for core in range(8):
        b, p = core // 2, core % 2
        o = np.asarray(results[core]["out"], dtype=np.float32).reshape(c.NOWN, 128, c.D)
        out[b].reshape(c.NB, 128, c.D)[p::2] = o
    return out


_NC_CACHE = {}


def kernel(**inputs):
    cfg = Cfg()
    if "nc" not in _NC_CACHE:
        _NC_CACHE["nc"] = build(cfg)
    nc = _NC_CACHE["nc"]
    maps = make_in_maps(cfg, inputs)
    res = run_bass_kernel_spmd(nc, maps, core_ids=list(range(8)))
    return assemble(cfg, res.results)
```
